# Optimizing a Trainium2 kernel written in Bass

```python
import math
import jax, jax.numpy as jnp
from jax import lax
import numpy as np

D_MODEL = 2048
BATCH = 16
SEQ = 2048
DEPTH = 1

GRID_W = 64
CTX_LEN = 256
HG_WIDTH = 1024
HY_WIDTH = D_MODEL - HG_WIDTH
HG_HEAD_DIM = 128
HG_HEADS = HG_WIDTH // HG_HEAD_DIM
HG_ROW_HEADS = HG_HEADS // 2
CHUNK = 64
SHORT_CONV = 3
N_BANDS = 16
FILTER_EMB = 1 + 2 * N_BANDS
FILTER_HIDDEN = 64
FILTER_TARGET = 1e-2
FAST_DECAY_PCT = 0.3
SLOW_DECAY_PCT = 1.5
FILTER_SHIFT = 0.05
D_FF = 4 * D_MODEL
N_MOD = 6
PROJ_WIDTH = 5 * HG_WIDTH + 3 * HY_WIDTH
EPS = 1e-6

kernel_name = "hgrn2_hyena_parallel_prefix_block"


def _rmsnorm(t, g):
    t32 = t.astype(jnp.float32)
    y = t32 * lax.rsqrt(jnp.mean(t32 * t32, axis=-1, keepdims=True) + EPS)
    return (y * g.astype(jnp.float32)).astype(t.dtype)


def _modulate(t, shift, scale):
    return t * (1.0 + scale) + shift


def _col_order(t, rows, inverse):
    b, l, h, d = t.shape
    nc = h - HG_ROW_HEADS
    grid = (GRID_W, rows) if inverse else (rows, GRID_W)
    col = t[:, :, HG_ROW_HEADS:].reshape(b, grid[0], grid[1], nc, d)
    col = jnp.swapaxes(col, 1, 2).reshape(b, l, nc, d)
    return jnp.concatenate([t[:, :, :HG_ROW_HEADS], col], axis=2)


def _chunk_scan(q, k, v, log_f, s0):
    b, l, h, dk = q.shape
    dv = v.shape[-1]
    n = l // CHUNK
    qc = q.reshape(b, n, CHUNK, h, dk)
    kc = k.reshape(b, n, CHUNK, h, dk)
    vc = v.reshape(b, n, CHUNK, h, dv)
    a = jnp.cumsum(log_f.reshape(b, n, CHUNK, h, dk), axis=2)
    a_end = a[:, :, -1:]
    a_mid = a[:, :, CHUNK // 2 - 1:CHUNK // 2]
    scores = jnp.einsum("bnihd,bnjhd->bnhij", qc * jnp.exp(a - a_mid), kc * jnp.exp(a_mid - a))
    tril = jnp.tril(jnp.ones((CHUNK, CHUNK), dtype=bool))
    scores = jnp.where(tril, scores, 0.0)
    o_intra = jnp.einsum("bnhij,bnjhe->bnihe", scores, vc)
    upd = jnp.einsum("bnjhd,bnjhe->bnhde", kc * jnp.exp(a_end - a), vc)
    decay = jnp.exp(a_end[:, :, 0])

    def step(s, xs):
        dec, u = xs
        return dec[..., None] * s + u, s

    s_fin, s_start = lax.scan(step, s0, (jnp.moveaxis(decay, 1, 0), jnp.moveaxis(upd, 1, 0)))
    s_start = jnp.moveaxis(s_start, 0, 1)
    o_inter = jnp.einsum("bnihd,bnhde->bnihe", qc * jnp.exp(a), s_start)
    return (o_intra + o_inter).reshape(b, l, h, dv), s_fin


def _hgrn2_mixer(p_lat, p_ctx, lb, norm_g, rows, need_ctx_out):
    f32 = jnp.float32
    lb_f = lb[0].reshape(HG_HEADS, HG_HEAD_DIM)
    lb_b = lb[1].reshape(HG_HEADS, HG_HEAD_DIM)

    def prep(p):
        p = p.astype(f32)
        b, l, _ = p.shape
        q, zf, zb, v, g = jnp.split(p, 5, axis=-1)
        hd = lambda t: t.reshape(b, l, HG_HEADS, HG_HEAD_DIM)
        f_f = lb_f + (1.0 - lb_f) * jax.nn.sigmoid(hd(zf))
        f_b = lb_b + (1.0 - lb_b) * jax.nn.sigmoid(hd(zb))
        return [hd(q), 1.0 - f_f, jnp.log(f_f), 1.0 - f_b, jnp.log(f_b), hd(v)], g

    (qc, kfc, lffc, kbc, lfbc, vc), gc = prep(p_ctx)
    lat, gl = prep(p_lat)
    ql, kfl, lffl, kbl, lfbl, vl = [_col_order(t, rows, False) for t in lat]
    b = qc.shape[0]
    zeros = jnp.zeros((b, HG_HEADS, HG_HEAD_DIM, HG_HEAD_DIM), f32)
    flip = lambda t: jnp.flip(t, axis=1)
    o_cf, s_cf = _chunk_scan(qc, kfc, vc, lffc, zeros)
    o_cb, s_cb = _chunk_scan(flip(qc), flip(kbc), flip(vc), flip(lfbc), zeros)
    o_lf, _ = _chunk_scan(ql, kfl, vl, lffl, s_cf)
    o_lb, _ = _chunk_scan(flip(ql), flip(kbl), flip(vl), flip(lfbl), s_cb)
    o_lat = _col_order(o_lf + flip(o_lb), rows, True)
    gain = norm_g.astype(f32).reshape(HG_HEADS, HG_HEAD_DIM)

    def readout(o, g):
        bb, l = o.shape[:2]
        o = o * lax.rsqrt(jnp.mean(o * o, axis=-1, keepdims=True) + EPS) * gain
        return o.reshape(bb, l, HG_WIDTH) * jax.nn.silu(g)

    out_ctx = readout(o_cf + flip(o_cb), gc) if need_ctx_out else None
    return readout(o_lat, gl), out_ctx


def _short_conv(t, w, b):
    half = SHORT_CONV // 2
    l = t.shape[1]
    tp = jnp.pad(t, ((0, 0), (half, half), (0, 0)))
    return sum(tp[:, j:j + l] * w[j] for j in range(SHORT_CONV)) + b


def _implicit_filter(l, w1, b1, freq, w2, b2, w3):
    pos = jnp.arange(l, dtype=jnp.float32)[:, None]
    t = pos / max(l - 1, 1)
    bands = jnp.linspace(1e-4, N_BANDS - 1, N_BANDS, dtype=jnp.float32)[None, :]
    ang = bands * (2.0 * math.pi) * pos / l
    z = jnp.concatenate([t, jnp.cos(ang), -jnp.sin(ang)], axis=-1)
    h = jnp.sin(freq * (z @ w1 + b1))
    h = jnp.sin(freq * (h @ w2 + b2))
    h = h @ w3
    deltas = jnp.abs(jnp.linspace(math.log(FILTER_TARGET) / SLOW_DECAY_PCT,
                                  math.log(FILTER_TARGET) / FAST_DECAY_PCT, HY_WIDTH, dtype=jnp.float32))
    deltas = jnp.tile(deltas, 2)
    return h * (jnp.exp(-t * deltas) + FILTER_SHIFT)


def _bidir_long_conv(u, filt, bias):
    l, ch = u.shape[1], u.shape[2]
    kern = jnp.concatenate([filt[:, :ch], jnp.zeros((1, ch), filt.dtype), filt[:0:-1, ch:]], axis=0)
    kern = kern * lax.rsqrt(jnp.sum(kern * kern, axis=0, keepdims=True))
    spec = jnp.fft.rfft(u, n=2 * l, axis=1) * jnp.fft.rfft(kern, axis=0)[None]
    y = jnp.fft.irfft(spec, n=2 * l, axis=1)[:, :l]
    return y + u * bias


def _hyena_mixer(p, conv_w, conv_b, w1, b1, freq, w2, b2, w3, bias):
    f32 = jnp.float32
    p = _short_conv(p.astype(f32), conv_w.astype(f32), conv_b.astype(f32))
    v, x1, x0 = jnp.split(p, 3, axis=-1)
    filt = _implicit_filter(p.shape[1], w1.astype(f32), b1.astype(f32), freq.astype(f32),
                            w2.astype(f32), b2.astype(f32), w3.astype(f32))
    return x0 * _bidir_long_conv(x1 * v, filt, bias.astype(f32))


def _sq_relu_mlp(u, w1, w2):
    return jnp.square(jax.nn.relu(u @ w1)) @ w2


def setup_inputs(seed: int = 0) -> dict:
    key = jax.random.key(seed)
    ks = jax.random.split(key, 24)
    nrm = lambda k, shape, scale: scale * jax.random.normal(k, shape, jnp.float32)
    return {
        "x": nrm(ks[0], (BATCH, SEQ, D_MODEL), 1.0),
        "c": nrm(ks[1], (BATCH, D_MODEL), 1.0),
        "ctx": nrm(ks[2], (BATCH, CTX_LEN, D_MODEL), 1.0),
        "c_ctx": nrm(ks[3], (D_MODEL,), 1.0),
        "w_ada": nrm(ks[4], (DEPTH, D_MODEL, N_MOD * D_MODEL), 0.5 * D_MODEL ** -0.5),
        "b_ada": nrm(ks[5], (DEPTH, N_MOD * D_MODEL), 0.02),
        "norm1_g": 1.0 + nrm(ks[6], (DEPTH, D_MODEL), 0.05),
        "w_in": nrm(ks[7], (DEPTH, D_MODEL, PROJ_WIDTH), D_MODEL ** -0.5),
        "hgrn_lb_logits": nrm(ks[8], (2, DEPTH + 1, HG_WIDTH), 0.5),
        "hgrn_norm_g": 1.0 + nrm(ks[9], (DEPTH, HG_WIDTH), 0.05),
        "hy_conv_w": nrm(ks[10], (DEPTH, SHORT_CONV, 3 * HY_WIDTH), SHORT_CONV ** -0.5),
        "hy_conv_b": nrm(ks[11], (DEPTH, 3 * HY_WIDTH), 0.02),
        "flt_w1": nrm(ks[12], (DEPTH, FILTER_EMB, FILTER_HIDDEN), FILTER_EMB ** -0.5),
        "flt_b1": nrm(ks[13], (DEPTH, FILTER_HIDDEN), 0.1),
        "flt_freq": 1.0 + nrm(ks[14], (DEPTH, FILTER_HIDDEN), 0.05),
        "flt_w2": nrm(ks[15], (DEPTH, FILTER_HIDDEN, FILTER_HIDDEN), FILTER_HIDDEN ** -0.5),
        "flt_b2": nrm(ks[16], (DEPTH, FILTER_HIDDEN), 0.1),
        "flt_w3": nrm(ks[17], (DEPTH, FILTER_HIDDEN, 2 * HY_WIDTH), FILTER_HIDDEN ** -0.5),
        "hy_bias": nrm(ks[18], (DEPTH, HY_WIDTH), 0.5),
        "w_out": nrm(ks[19], (DEPTH, D_MODEL, D_MODEL), D_MODEL ** -0.5),
        "norm2_g": 1.0 + nrm(ks[20], (DEPTH, D_MODEL), 0.05),
        "w_mlp1": nrm(ks[21], (DEPTH, D_MODEL, D_FF), D_MODEL ** -0.5),
        "w_mlp2": nrm(ks[22], (DEPTH, D_FF, D_MODEL), D_FF ** -0.5),
        "final_norm_g": 1.0 + nrm(ks[23], (D_MODEL,), 0.05),
    }


def reference(x, c, ctx, c_ctx, w_ada, b_ada, norm1_g, w_in, hgrn_lb_logits, hgrn_norm_g,
              hy_conv_w, hy_conv_b, flt_w1, flt_b1, flt_freq, flt_w2, flt_b2, flt_w3, hy_bias,
              w_out, norm2_g, w_mlp1, w_mlp2, final_norm_g):
    rows = x.shape[1] // GRID_W
    hg_cols = 5 * HG_WIDTH
    lb_all = jnp.cumsum(jax.nn.softmax(hgrn_lb_logits.astype(jnp.float32), axis=1), axis=1)
    silu_c = jax.nn.silu(c)
    silu_cc = jax.nn.silu(c_ctx)
    for layer in range(DEPTH):
        need_ctx = layer + 1 < DEPTH
        m_lat = (silu_c @ w_ada[layer] + b_ada[layer])[:, None, :]
        m_ctx = silu_cc @ w_ada[layer] + b_ada[layer]
        sh1, sc1, g1, sh2, sc2, g2 = jnp.split(m_lat, N_MOD, axis=-1)
        csh1, csc1, cg1, csh2, csc2, cg2 = jnp.split(m_ctx, N_MOD, axis=-1)

        u_lat = _modulate(_rmsnorm(x, norm1_g[layer]), sh1, sc1)
        u_ctx = _modulate(_rmsnorm(ctx, norm1_g[layer]), csh1, csc1)
        p_lat = u_lat @ w_in[layer]
        p_ctx = u_ctx @ w_in[layer]

        a_lat, a_ctx = _hgrn2_mixer(p_lat[..., :hg_cols], p_ctx[..., :hg_cols], lb_all[:, layer],
                                    hgrn_norm_g[layer], rows, need_ctx)
        hy_args = (hy_conv_w[layer], hy_conv_b[layer], flt_w1[layer], flt_b1[layer], flt_freq[layer],
                   flt_w2[layer], flt_b2[layer], flt_w3[layer], hy_bias[layer])
        y_lat = _hyena_mixer(p_lat[..., hg_cols:], *hy_args)

        mix_lat = jnp.concatenate([a_lat, y_lat], axis=-1).astype(x.dtype) @ w_out[layer]
        x = x + g1 * mix_lat
        x = x + g2 * _sq_relu_mlp(_modulate(_rmsnorm(x, norm2_g[layer]), sh2, sc2), w_mlp1[layer], w_mlp2[layer])

        if need_ctx:
            y_ctx = _hyena_mixer(p_ctx[..., hg_cols:], *hy_args)
            mix_ctx = jnp.concatenate([a_ctx, y_ctx], axis=-1).astype(ctx.dtype) @ w_out[layer]
            ctx = ctx + cg1 * mix_ctx
            ctx = ctx + cg2 * _sq_relu_mlp(_modulate(_rmsnorm(ctx, norm2_g[layer]), csh2, csc2),
                                           w_mlp1[layer], w_mlp2[layer])
    return _rmsnorm(x, final_norm_g)
```

```python
import os
from contextlib import ExitStack
import numpy as np
import ml_dtypes
import concourse.bass as bass
import concourse.mybir as mybir
from concourse.bass_utils import run_bass_kernel_spmd

F32 = mybir.dt.float32
BF16 = mybir.dt.bfloat16
AF = mybir.ActivationFunctionType
ALU = mybir.AluOpType

D = 2048
L = 2048
CL = 256
NB = 2
EPS = 1e-6
PI = float(np.pi)
DEBUG = bool(int(os.environ.get("MK_DEBUG", "0")))

R_C, R_BADA, R_N1, R_LB, R_HG, R_CW, R_CB, R_HB, R_N2, R_FN, R_FLT = 0, 48, 144, 160, 192, 200, 272, 296, 304, 320, 336
NROWS = 384

ENGS = ("pe", "act", "dve", "pool", "sp")
EPOCH = 30000
NSEM_ENG = 4
NDMA_SEM = 12


class Buf:
    __slots__ = ("name", "last_writer", "readers")

    def __init__(self, name):
        self.name = name
        self.last_writer = None
        self.readers = []


class Op:
    __slots__ = ("idx", "eng", "fn", "deps", "is_dma", "has_dep", "sem", "val")

    def __init__(self, idx, eng, fn, is_dma):
        self.idx = idx
        self.eng = eng
        self.fn = fn
        self.deps = set()
        self.is_dma = is_dma
        self.has_dep = False
        self.sem = None
        self.val = None


class Sched:
    def __init__(self, nc):
        self.nc = nc
        self.ops = []
        self.final_waits = []
        self.last_eng = {}
        self.dma_slots = {}
        self.dma_cnt = {"sp": 0, "pool": 0, "act": 0}
        self.fence = set()

    def buf(self, name):
        return Buf(name)

    def bufs(self, name, n):
        return [Buf(f"{name}{i}") for i in range(n)]

    def phase_fence(self):
        self.fence = set(self.last_eng.values()) | set(self.dma_slots.values())

    def op(self, eng, fn, reads=(), writes=(), is_dma=False, final=False):
        o = Op(len(self.ops), eng, fn, is_dma)
        for b in reads:
            if b.last_writer is not None:
                o.deps.add(b.last_writer)
        for b in writes:
            if b.last_writer is not None:
                o.deps.add(b.last_writer)
            for r in b.readers:
                o.deps.add(r)
        for b in reads:
            if not is_dma:
                b.readers = [r for r in b.readers if r != o.idx and (self.ops[r].is_dma or self.ops[r].eng != eng)]
            b.readers.append(o.idx)
        for b in writes:
            b.last_writer = o.idx
            b.readers = []
        o.deps |= self.fence
        if is_dma:
            slot = (eng, self.dma_cnt[eng] % NDMA_SEM)
            self.dma_cnt[eng] += 1
            if slot in self.dma_slots:
                o.deps.add(self.dma_slots[slot])
            self.dma_slots[slot] = o.idx
        else:
            self.last_eng[eng] = o.idx
        o.deps.discard(o.idx)
        self.ops.append(o)
        if final:
            self.final_waits.append(o.idx)
        return o

    def emit(self, stack):
        nc = self.nc
        ops = self.ops
        for o in ops:
            if o.eng == "pe" and not o.is_dma:
                o.deps = {d for d in o.deps if not (ops[d].eng == "pe" and not ops[d].is_dma)}
            for d in o.deps:
                ops[d].has_dep = True
        for i in self.final_waits:
            ops[i].has_dep = True
        tick_sems = {e: [stack.enter_context(nc.semaphore(f"t_{e}{k}")) for k in range(NSEM_ENG)] for e in ENGS}
        dma_sems = {e: [stack.enter_context(nc.semaphore(f"d_{e}{k}")) for k in range(NDMA_SEM)] for e in ("sp", "pool")}
        tick = {e: 0 for e in ENGS}
        dcnt = {e: 0 for e in dma_sems}
        dval = {}
        for o in ops:
            if o.is_dma:
                j = dcnt[o.eng] % NDMA_SEM
                dcnt[o.eng] += 1
                key = (o.eng, j)
                dval[key] = dval.get(key, 0) + 16
                o.sem = dma_sems[o.eng][j]
                o.val = dval[key]
                o.has_dep = True
            elif o.has_dep:
                t = tick[o.eng]
                tick[o.eng] += 1
                o.sem = tick_sems[o.eng][t // EPOCH]
                o.val = t % EPOCH + 1
        self.stats = dict(n_ops=len(ops), ticks=dict(tick), dmas=dict(dcnt))
        per_eng = {e: [o for o in ops if o.eng == e] for e in ENGS}
        final = [ops[i] for i in self.final_waits]

        def run(eng_name, eng):
            waited = {}

            def wait_for(p):
                k = id(p.sem)
                if waited.get(k, 0) >= p.val:
                    return
                eng.wait_ge(p.sem, p.val)
                waited[k] = p.val

            for o in per_eng[eng_name]:
                for d in sorted(o.deps):
                    wait_for(ops[d])
                ins = o.fn(eng)
                if o.has_dep:
                    ins.then_inc(o.sem, 16 if o.is_dma else 1)
            if eng_name == "sp":
                for p in final:
                    wait_for(p)

        with nc.Block() as block:
            @block.tensor
            def _(e):
                run("pe", e)

            @block.scalar
            def _(e):
                run("act", e)

            @block.vector
            def _(e):
                run("dve", e)

            @block.gpsimd
            def _(e):
                run("pool", e)

            @block.sync
            def _(e):
                run("sp", e)


_CONST = {}


def _consts():
    if _CONST:
        return _CONST
    bf = ml_dtypes.bfloat16
    c = {}
    c["identb"] = np.eye(128, dtype=np.float32).astype(bf)
    c["identf"] = np.eye(128, dtype=np.float32)
    j = np.arange(128)
    same = (j[:, None] // 64) == (j[None, :] // 64)
    lj = (j % 64)[:, None]
    lx = (j % 64)[None, :]
    A_f = same & (lj <= lx)
    A_b = same & (lj >= lx)
    Mid_f = same & (lj <= 31)
    Mid_b = same & (lj >= 32)
    End = same
    f = lambda m: m.astype(np.float32)
    mats = [f(A_f), f(A_b), f(A_f) - f(Mid_f), f(A_b) - f(Mid_b), f(Mid_f) - f(A_f), f(Mid_b) - f(A_b),
            f(End) - f(A_f), f(End) - f(A_b), np.ones((128, 128), np.float32)]
    c["cmats"] = np.ascontiguousarray(np.stack(mats, axis=1))
    pos = np.arange(L, dtype=np.float32)[:, None]
    t = pos / np.float32(max(L - 1, 1))
    bands = np.linspace(1e-4, 16 - 1, 16, dtype=np.float32)[None, :]
    ang = bands * np.float32(2.0 * np.pi) * pos / np.float32(L)
    z = np.concatenate([t, np.cos(ang), -np.sin(ang)], axis=-1).astype(np.float32)
    c["zT"] = np.ascontiguousarray(z.T)
    deltas = np.abs(np.linspace(np.log(1e-2) / 1.5, np.log(1e-2) / 0.3, 1024, dtype=np.float32))
    deltas = np.tile(deltas, 2).astype(np.float32)
    c["deltas"] = np.ascontiguousarray(np.broadcast_to(deltas[None, :], (128, 2048)))
    c["negt"] = np.ascontiguousarray((-t[:, 0]).reshape(16, 128).T)
    s = np.arange(2048, dtype=np.float64)[:, None]
    g = np.arange(2048, dtype=np.float64)[None, :]
    th = 2.0 * np.pi * ((s * g) % 4096) / 4096.0
    C = np.cos(th)
    Sn = np.sin(th)
    FT = np.empty((2048, 4096), np.float64)
    FT[:, :2048] = C
    FT[:, 2048:] = -Sn
    FT[:, 2048] = (-1.0) ** np.arange(2048)
    FTt = FT.reshape(16, 128, 2, 16, 128).transpose(3, 1, 0, 2, 4)
    c["FTt"] = np.ascontiguousarray(FTt).astype(bf)
    GT = np.empty((4096, 2048), np.float64)
    GT[:2048] = C * (2.0 / 4096.0)
    GT[2048:] = -Sn * (2.0 / 4096.0)
    GT[0] = 1.0 / 4096.0
    GT[2048] = ((-1.0) ** np.arange(2048)) / 4096.0
    GTt = GT.reshape(32, 128, 8, 256).transpose(2, 1, 0, 3)
    c["GTt"] = np.ascontiguousarray(GTt).astype(bf)
    _CONST.update(c)
    return _CONST


def build_program(debug=False):
    nc = bass.Bass("TRN2", target_bir_lowering=False)
    din = lambda n, s, d=F32: nc.dram_tensor(n, list(s), d, kind="ExternalInput").ap()
    dscr = lambda n, s, d=F32: nc.dram_tensor(n, list(s), d, kind=("ExternalOutput" if debug else "Internal")).ap()
    x_d = din("x", [NB * L, D])
    ctx_d = din("ctx", [NB * CL, D])
    vecs_d = din("vecs", [NROWS, 128])
    wada_d = din("w_ada", [D, 6 * D])
    bada_d = din("b_ada", [1, 6 * D])
    lbl_d = din("lbl", [1, 4096])
    win_d = din("w_in", [D, 8192])
    wout_d = din("w_out", [D, D])
    w1_d = din("w_mlp1", [D, 8192])
    w2_d = din("w_mlp2", [8192, D])
    fw1_d = din("flt_w1", [33, 64])
    fw2_d = din("flt_w2", [64, 64])
    fw3_d = din("flt_w3", [64, 2048])
    fng_d = din("fng", [1, D])
    identb_d = din("identb", [128, 128], BF16)
    identf_d = din("identf", [128, 128])
    cmats_d = din("cmats", [128, 9, 128])
    zT_d = din("zT", [33, 2048])
    deltas_d = din("deltas", [128, 2048])
    negt_d = din("negt", [128, 16])
    FTt_d = din("FTt", [16, 128, 16, 2, 128], BF16)
    GTt_d = din("GTt", [8, 128, 32, 256], BF16)
    out_d = nc.dram_tensor("out", [NB * L, D], F32, kind="ExternalOutput").ap()
    mrow_d = dscr("mrow", [3, 6 * D])
    oml_d = dscr("omlD", [1, 2048])
    ksp_d = dscr("kspec", [16, 128, 2, 1024])
    mix_d = dscr("mixD", [NB, D, L], BF16)
    winb_d = nc.dram_tensor("winb", [D, 8192], BF16).ap()
    woutb_d = nc.dram_tensor("woutb", [D, D], BF16).ap()
    w1b_d = nc.dram_tensor("w1b", [D, 8192], BF16).ap()
    w2b_d = nc.dram_tensor("w2b", [8192, D], BF16).ap()

    st = ExitStack()
    with st:
        S = Sched(nc)
        sbt = lambda n, s, d: st.enter_context(nc.sbuf_tensor(n, list(s), d))
        A1_BYTES = 72 * 1024
        A2_BYTES = 124 * 1024
        arena1 = sbt("arena1", [128, A1_BYTES // 2], BF16)
        arena2 = sbt("arena2", [128, A2_BYTES // 2], BF16)
        identb = sbt("identb_s", [128, 128], BF16)
        identf = sbt("identf_s", [128, 128], F32)
        cm = sbt("cm_s", [128, 9, 128], F32)
        vT = sbt("vT", [128, NROWS], F32)
        mT = sbt("mT", [128, 96, 3], F32)
        sc1p = sbt("sc1p", [128, 16, 3], F32)
        sc2p = sbt("sc2p", [128, 16, 2], F32)
        Sst = sbt("Sst", [128, 2, 128], F32)
        Sbf = sbt("Sbf", [128, 2, 128], BF16)
        small = sbt("small", [128, 64], F32)
        banks = [st.enter_context(nc.psum_tensor(f"bank{i}", [128, 512], F32)) for i in range(8)]
        B_bank = S.bufs("bank", 8)
        B_const = S.buf("const")
        B_vT = S.buf("vT")
        B_mT = S.buf("mT")
        B_scp = S.buf("scp")
        B_small = S.buf("small")
        ONES = cm[:, 8, :]

        def carve(arena, off, shape, dtype, parts=128):
            esz = 2 if dtype == BF16 else 4
            n = int(np.prod(shape))
            assert off % 4 == 0
            lim = A1_BYTES if arena is arena1 else A2_BYTES
            assert off + n * esz <= lim, (off, n * esz, lim)
            a = arena[:, off // 2: off // 2 + n * esz // 2]
            if dtype != BF16:
                a = a.bitcast(dtype)
            if len(shape) == 2:
                a = a.rearrange("p (a b) -> p a b", a=shape[0])
            elif len(shape) == 3:
                a = a.rearrange("p (a b c) -> p a b c", a=shape[0], b=shape[1])
            elif len(shape) == 4:
                a = a.rearrange("p (a b c d) -> p a b c d", a=shape[0], b=shape[1], c=shape[2])
            if parts != 128:
                a = a[0:parts]
            return a

        class Alloc:
            def __init__(self, arena):
                self.arena = arena
                self.off = 0

            def __call__(self, shape, dtype, parts=128):
                esz = 2 if dtype == BF16 else 4
                a = carve(self.arena, self.off, shape, dtype, parts)
                self.off += (int(np.prod(shape)) * esz + 31) // 32 * 32
                return a

        def dma(eng, out, in_, reads=(), writes=(), final=False):
            S.op(eng, lambda e: e.dma_start(out=out, in_=in_), reads, writes, is_dma=True, final=final)

        def mm(out, lhsT, rhs, start, stop, reads, writes, tp=None):
            if tp is None:
                S.op("pe", lambda e: e.matmul(out, lhsT=lhsT, rhs=rhs, start=start, stop=stop), reads, writes)
            else:
                S.op("pe", lambda e: e.matmul(out, lhsT=lhsT, rhs=rhs, start=start, stop=stop, tile_position=tp), reads, writes)

        def tr(out, in_, ident, reads, writes):
            S.op("pe", lambda e: e.transpose(out, in_, ident), reads, writes)

        def act(out, in_, func, reads, writes, scale=1.0, bias=0.0, accum=None):
            if accum is None:
                S.op("act", lambda e: e.activation(out=out, in_=in_, func=func, bias=bias, scale=scale), reads, writes)
            else:
                S.op("act", lambda e: e.activation(out=out, in_=in_, func=func, bias=bias, scale=scale, accum_out=accum), reads, writes)

        def tt(eng, out, in0, in1, op, reads, writes):
            S.op(eng, lambda e: e.tensor_tensor(out=out, in0=in0, in1=in1, op=op), reads, writes)

        def ts(eng, out, in0, s1, s2, op0, op1, reads, writes):
            if s2 is None:
                S.op(eng, lambda e: e.tensor_scalar(out=out, in0=in0, scalar1=s1, scalar2=None, op0=op0), reads, writes)
            else:
                S.op(eng, lambda e: e.tensor_scalar(out=out, in0=in0, scalar1=s1, scalar2=s2, op0=op0, op1=op1), reads, writes)

        def stt(eng, out, in0, scalar, in1, op0, op1, reads, writes):
            S.op(eng, lambda e: e.scalar_tensor_tensor(out=out, in0=in0, scalar=scalar, in1=in1, op0=op0, op1=op1), reads, writes)

        def cp(eng, out, in_, reads, writes):
            if eng == "act":
                S.op("act", lambda e: e.copy(out=out, in_=in_), reads, writes)
            else:
                S.op(eng, lambda e: e.tensor_copy(out, in_), reads, writes)

        def recip(out, in_, reads, writes):
            S.op("dve", lambda e: e.reciprocal(out=out, in_=in_), reads, writes)

        def memset(eng, ap, val, writes):
            S.op(eng, lambda e: e.memset(ap, val), (), writes)

        def rstd_from_ssq(ssq, tmp, rstd, n, B_s):
            act(tmp, ssq, AF.Sqrt, [B_s], [B_s], scale=1.0 / n, bias=EPS)
            recip(rstd, tmp, [B_s], [B_s])

        dma("sp", identb[:], identb_d, writes=[B_const])
        dma("sp", identf[:], identf_d, writes=[B_const])
        dma("sp", cm[:], cmats_d, writes=[B_const])

        B_winb = S.bufs("winb", 16)
        B_woutb = S.bufs("woutb", 4)
        B_w1b = S.bufs("w1b", 16)
        B_w2b = S.bufs("w2b", 8)
        for ch in (0, 2, 4, 6, 8, 1, 3, 5, 7, 9, 10, 11, 12, 13, 14, 15):
            dma("pool", winb_d[:, ch * 512:(ch + 1) * 512], win_d[:, ch * 512:(ch + 1) * 512], writes=[B_winb[ch]])
        pending = []
        for ch in range(4):
            pending.append((woutb_d[:, ch * 512:(ch + 1) * 512], wout_d[:, ch * 512:(ch + 1) * 512], B_woutb[ch]))
        for ch in range(16):
            pending.append((w1b_d[:, ch * 512:(ch + 1) * 512], w1_d[:, ch * 512:(ch + 1) * 512], B_w1b[ch]))
        for ch in range(8):
            pending.append((w2b_d[ch * 1024:(ch + 1) * 1024, :], w2_d[ch * 1024:(ch + 1) * 1024, :], B_w2b[ch]))

        def issue_precast(n):
            for _ in range(n):
                if pending:
                    o_, i_, B_ = pending.pop(0)
                    dma("pool", o_, i_, writes=[B_])

        a2 = Alloc(arena2)
        vr = a2([3, 128], F32)
        B_vr = S.buf("vr")
        dma("sp", vr, vecs_d.rearrange("(t p) c -> p t c", p=128), writes=[B_vr])
        for t in range(3):
            tr(banks[0][:, t * 128:(t + 1) * 128], vr[:, t, :], identf[:], [B_vr, B_const], [B_bank[0]])
        cp("dve", vT[:], banks[0][:, 0:384], [B_bank[0]], [B_vT])
        siluT = a2([48], F32)
        B_silu = S.buf("silu")
        act(siluT, vT[:, R_C:R_C + 48], AF.Silu, [B_vT], [B_silu])
        siluT3 = siluT.rearrange("p (j k) -> p k j", k=16)

        wa = [a2([16, 512], F32) for _ in range(2)]
        B_wa = S.bufs("wa", 2)
        brow = [a2([512], F32, parts=3) for _ in range(2)]
        B_brow = S.bufs("brow", 2)
        mrs = [a2([512], F32, parts=3) for _ in range(2)]
        B_mrs = S.bufs("mrs", 2)
        B_mrowD = S.buf("mrowD")
        for cc in range(24):
            k = cc % 2
            cs = slice(cc * 512, (cc + 1) * 512)
            dma("sp", wa[k], wada_d[:, cs].rearrange("(kt p) n -> p kt n", p=128), writes=[B_wa[k]])
            dma("sp", brow[k], bada_d[0:1, cs].broadcast_to([3, 512]), writes=[B_brow[k]])
            pb = banks[cc % 2]
            for kt in range(16):
                mm(pb[0:3, :], siluT3[:, kt, :], wa[k][:, kt, :], kt == 0, kt == 15, [B_silu, B_wa[k]], [B_bank[cc % 2]])
            tt("dve", mrs[k], pb[0:3, :], brow[k], ALU.add, [B_bank[cc % 2], B_brow[k]], [B_mrs[k]])
            dma("sp", mrow_d[:, cs], mrs[k], reads=[B_mrs[k]], writes=[B_mrowD])
            pt = banks[2 + cc % 2]
            for q in range(4):
                tr(pt[:, q * 3:(q + 1) * 3], mrs[k][:, q * 128:(q + 1) * 128], identf[0:3, 0:3], [B_mrs[k], B_const], [B_bank[2 + cc % 2]])
            cp("act", mT[:, cc * 4:(cc + 1) * 4, :], pt[:, 0:12].rearrange("p (q j) -> p q j", j=3), [B_bank[2 + cc % 2]], [B_mT])
        for j in range(3):
            stt("dve", sc1p[:, :, j], mT[:, 16:32, j], 1.0, vT[:, R_N1:R_N1 + 16], ALU.add, ALU.mult, [B_mT, B_vT], [B_scp])
        for j in range(2):
            stt("dve", sc2p[:, :, j], mT[:, 64:80, j], 1.0, vT[:, R_N2:R_N2 + 16], ALU.add, ALU.mult, [B_mT, B_vT], [B_scp])

        lrow = a2([4096], F32, parts=1)
        B_lrow = S.buf("lrow")
        B_omlD = S.buf("omlD")
        dma("sp", lrow, lbl_d, writes=[B_lrow])
        l4 = lrow.rearrange("p (d s c) -> p d s c", d=2, s=2)
        tt("dve", l4[:, :, 0, :], l4[:, :, 0, :], l4[:, :, 1, :], ALU.subtract, [B_lrow], [B_lrow])
        act(l4[:, :, 1, :], l4[:, :, 0, :], AF.Sigmoid, [B_lrow], [B_lrow], scale=-1.0)
        dma("sp", oml_d.rearrange("o (d c) -> o d c", d=2), l4[:, :, 1, :], reads=[B_lrow], writes=[B_omlD])

        S.phase_fence()
        a2 = Alloc(arena2)
        a1 = Alloc(arena1)
        zT = a2([2048], F32, parts=33)
        fw1 = a2([64], F32, parts=33)
        fw2 = a2([64], F32, parts=64)
        fw3 = a2([2048], F32, parts=64)
        h1T = a2([2048], F32, parts=64)
        h2T = a2([2048], F32, parts=64)
        deltas = a2([2048], F32)
        negt = a2([16], F32)
        ytmp = a2([512], F32, parts=64)
        wtmp = a2([512], F32, parts=64)
        filt = a2([2048], F32)
        sq = a2([2048], F32)
        rn = a2([1024], F32)
        kst = a2([2, 1024], F32)
        ftb = [a2([16, 2, 128], BF16) for _ in range(2)]
        ksum = a1([16, 1024], BF16)
        kdiff = a1([16, 1024], BF16)
        B_f = S.buf("fconst")
        B_h1, B_h2, B_yt, B_wt, B_filt, B_sq, B_rn, B_kst = (S.buf(n) for n in ("h1", "h2", "yt", "wt", "filt", "sq", "rn", "kst"))
        B_ftb = S.bufs("ftb", 2)
        B_ks = S.buf("ksum")
        B_kspD = S.buf("kspD")
        dma("sp", zT, zT_d, writes=[B_f])
        dma("sp", fw1, fw1_d, writes=[B_f])
        dma("sp", fw2, fw2_d, writes=[B_f])
        dma("sp", fw3, fw3_d, writes=[B_f])
        dma("sp", deltas, deltas_d, writes=[B_f])
        dma("sp", negt, negt_d, writes=[B_f])
        FB1, FRQ, FB2 = small[0:64, 0:1], vT[0:64, R_FLT + 1:R_FLT + 2], small[0:64, 1:2]
        tt("dve", FB1, vT[0:64, R_FLT:R_FLT + 1], FRQ, ALU.mult, [B_vT], [B_small])
        tt("dve", FB2, vT[0:64, R_FLT + 2:R_FLT + 3], FRQ, ALU.mult, [B_vT], [B_small])

        def sin_layer(dst, B_dst, lhsT, rhs_full, B_rhs, fb):
            for c4 in range(4):
                pb = banks[c4 % 2]
                cs = slice(c4 * 512, (c4 + 1) * 512)
                mm(pb[0:64, :], lhsT, rhs_full[:, cs], True, True, [B_f, B_rhs], [B_bank[c4 % 2]])
                ts("dve", ytmp, pb[0:64, :], FRQ, fb, ALU.mult, ALU.add, [B_bank[c4 % 2], B_vT, B_small], [B_yt])
                for _ in range(2):
                    ts("dve", wtmp, ytmp, PI, -2 * PI, ALU.is_gt, ALU.mult, [B_yt], [B_wt])
                    tt("dve", ytmp, ytmp, wtmp, ALU.add, [B_yt, B_wt], [B_yt])
                    ts("dve", wtmp, ytmp, -PI, 2 * PI, ALU.is_lt, ALU.mult, [B_yt], [B_wt])
                    tt("dve", ytmp, ytmp, wtmp, ALU.add, [B_yt, B_wt], [B_yt])
                act(dst[:, cs], ytmp, AF.Sin, [B_yt], [B_dst])

        sin_layer(h1T, B_h1, fw1, zT, B_f, FB1)
        sin_layer(h2T, B_h2, fw2, h1T, B_h1, FB2)
        for jt in range(16):
            for c4 in range(4):
                mm(banks[c4][:, :], h2T[:, jt * 128:(jt + 1) * 128], fw3[:, c4 * 512:(c4 + 1) * 512], True, True, [B_h2, B_f], [B_bank[c4]])
            act(sq, deltas, AF.Exp, [B_f], [B_sq], scale=negt[:, jt:jt + 1])
            for c4 in range(4):
                cs = slice(c4 * 512, (c4 + 1) * 512)
                stt("dve", filt[:, cs], sq[:, cs], 0.05, banks[c4][:, :], ALU.add, ALU.mult, [B_sq, B_bank[c4]], [B_filt])
            if jt == 0:
                memset("dve", filt[0:1, 1024:2048], 0.0, [B_filt])
            tt("dve", ksum[:, jt, :], filt[:, 0:1024], filt[:, 1024:2048], ALU.add, [B_filt], [B_ks])
            tt("pool", kdiff[:, jt, :], filt[:, 0:1024], filt[:, 1024:2048], ALU.subtract, [B_filt], [B_ks])
            act(sq, filt, AF.Square, [B_filt], [B_sq])
            for c4 in range(4):
                mm(banks[4 + c4][:, :], ONES, sq[:, c4 * 512:(c4 + 1) * 512], jt == 0, jt == 15, [B_const, B_sq], [B_bank[4 + c4]])
        for c2 in range(2):
            cp("act", rn[:, c2 * 512:(c2 + 1) * 512], banks[4 + c2][:, :], [B_bank[4 + c2]], [B_rn])
            tt("dve", rn[:, c2 * 512:(c2 + 1) * 512], rn[:, c2 * 512:(c2 + 1) * 512], banks[6 + c2][:, :], ALU.add, [B_rn, B_bank[6 + c2]], [B_rn])
        act(rn, rn, AF.Sqrt, [B_rn], [B_rn])
        recip(rn, rn, [B_rn], [B_rn])
        for T in range(16):
            k = T % 2
            dma("sp", ftb[k], FTt_d[T], writes=[B_ftb[k]])
            for ri in range(2):
                src = ksum if ri == 0 else kdiff
                for c2 in range(2):
                    bi = ri * 2 + c2
                    for s_ in range(16):
                        mm(banks[bi][:, :], ftb[k][:, s_, ri, :], src[:, s_, c2 * 512:(c2 + 1) * 512], s_ == 0, s_ == 15, [B_ftb[k], B_ks], [B_bank[bi]])
            if T == 0:
                for c2 in range(2):
                    for s_ in range(16):
                        mm(banks[4 + c2][0:1, :], ftb[k][:, s_, 1, 0:1], ksum[:, s_, c2 * 512:(c2 + 1) * 512], s_ == 0, s_ == 15, [B_ftb[k], B_ks], [B_bank[4 + c2]])
            for ri in range(2):
                for c2 in range(2):
                    bi = ri * 2 + c2
                    tt("dve", kst[:, ri, c2 * 512:(c2 + 1) * 512], banks[bi][:, :], rn[:, c2 * 512:(c2 + 1) * 512], ALU.mult, [B_bank[bi], B_rn], [B_kst])
            if T == 0:
                for c2 in range(2):
                    tt("dve", kst[0:1, 1, c2 * 512:(c2 + 1) * 512], banks[4 + c2][0:1, :], rn[0:1, c2 * 512:(c2 + 1) * 512], ALU.mult, [B_bank[4 + c2], B_rn], [B_kst])
            dma("sp", ksp_d[T], kst, reads=[B_kst], writes=[B_kspD])

        B_mixD = S.buf("mixD")
        for b in range(NB):
            S.phase_fence()
            a1 = Alloc(arena1)
            uT = a1([16, L], BF16)
            ucT = a1([16, CL], BF16)
            B_uT = S.buf("uT")
            a2 = Alloc(arena2)
            xs = [a2([D], F32) for _ in range(2)]
            xn = [a2([D], BF16) for _ in range(2)]
            junk = a2([D], BF16)
            B_xs = S.bufs("xs", 2)
            B_xn = S.bufs("xn", 2)
            B_junk = S.buf("junk")

            def norm_transpose(src_rows, dstT, col0, j, it, B_dst, sc, sh_base):
                k = it % 2
                dma("sp", xs[k], src_rows, writes=[B_xs[k]])
                sm = small[:, 8 + 4 * k: 12 + 4 * k]
                memset("dve", sm[:, 0:1], 0.0, [B_small])
                act(junk, xs[k], AF.Square, [B_xs[k]], [B_junk, B_small], accum=sm[:, 0:1])
                rstd_from_ssq(sm[:, 0:1], sm[:, 1:2], sm[:, 2:3], D, B_small)
                ts("dve", xn[k], xs[k], sm[:, 2:3], None, ALU.mult, None, [B_xs[k], B_small], [B_xn[k]])
                for half in range(2):
                    bi = 2 * k + half
                    pbf = banks[bi][:].bitcast(BF16)
                    for q in range(8):
                        dt = half * 8 + q
                        tr(pbf[:, q * 128:(q + 1) * 128], xn[k][:, dt * 128:(dt + 1) * 128], identb[:], [B_xn[k], B_const], [B_bank[bi]])
                    for q in range(8):
                        dt = half * 8 + q
                        o_ = dstT[:, dt, col0:col0 + 128]
                        i_ = pbf[:, q * 128:(q + 1) * 128]
                        if q % 2 == 0:
                            act(o_, i_, AF.Identity, [B_bank[bi], B_scp, B_mT], [B_dst], scale=sc[:, dt, j:j + 1], bias=mT[:, sh_base + dt, j:j + 1])
                        else:
                            ts("dve", o_, i_, sc[:, dt, j:j + 1], mT[:, sh_base + dt, j:j + 1], ALU.mult, ALU.add, [B_bank[bi], B_scp, B_mT], [B_dst])

            it = 0
            for tl in range(2):
                norm_transpose(ctx_d[b * CL + tl * 128: b * CL + (tl + 1) * 128, :], ucT, tl * 128, 2, it, B_uT, sc1p, 0)
                it += 1
            for tl in range(16):
                norm_transpose(x_d[b * L + tl * 128: b * L + (tl + 1) * 128, :], uT, tl * 128, b, it, B_uT, sc1p, 0)
                it += 1

            S.phase_fence()
            a2 = Alloc(arena2)
            wh = a2([16, 5, 128], BF16)
            omlrow = a2([2, 128], F32)
            cmb = a2([8, 128], BF16)
            ktok_f = a2([18 * 256], F32)
            lftok_f = a2([18 * 256], F32)
            ktok = ktok_f.rearrange("p (t d c) -> p t d c", t=18, d=2)
            lftok = lftok_f.rearrange("p (t d c) -> p t d c", t=18, d=2)
            vtok = a2([18, 128], BF16)
            qT = a2([L], F32)
            gT = a2([L], F32)
            osum = a2([L], F32)
            etmp = [a2([256], F32)] * 2
            lfh = [a2([128], BF16) for _ in range(6)]
            lfl = [a2([128], BF16) for _ in range(6)]
            Eb = [a2([4, 128], F32) for _ in range(6)]
            Khat = [a2([128], BF16) for _ in range(6)]
            Ktil = [a2([128], BF16) for _ in range(6)]
            KtilT = [a2([128], BF16) for _ in range(6)]
            QtilT = [a2([128], BF16) for _ in range(6)]
            QaT = [a2([128], BF16) for _ in range(6)]
            scm = [a2([128], BF16) for _ in range(6)]
            rtmp = a2([512], F32)
            astage = a2([L], BF16)
            B_wh, B_oml, B_ktok, B_lf, B_vtok, B_qT, B_gT, B_osum, B_rtmp, B_ast = (S.buf(n) for n in
                ("wh", "oml", "ktok", "lf", "vtok", "qT", "gT", "osum", "rtmp", "ast"))
            B_et = [S.buf("et")] * 2
            B_lfh = S.bufs("lfh", 6)
            B_cmb = S.buf("cmb")
            B_E = S.bufs("E", 6)
            B_Kh = S.bufs("Kh", 6)
            B_Kt = S.bufs("Kt", 6)
            B_KtT = S.bufs("KtT", 6)
            B_Qt = S.bufs("Qt", 6)
            B_Qa = S.bufs("Qa", 6)
            B_scm = S.bufs("scm", 6)
            B_S = S.bufs("S", 2)
            B_Sb = S.bufs("Sb", 2)
            B_ct = S.bufs("bct", 2)
            B_cs = S.bufs("bcs", 2)
            B_cu = S.bufs("bcu", 2)
            for dr_ in range(2):
                for q_, src_ in enumerate((2, 0, 4, 6)):
                    cp("dve", cmb[:, 4 * dr_ + q_, :], cm[:, src_ + dr_, :], [B_const], [B_cmb])

            for h in range(8):
                colmaj = h >= 4
                dma("sp", omlrow, oml_d[0:1, :].rearrange("o (d c) -> o d c", d=2)[:, :, h * 128:(h + 1) * 128].broadcast_to([128, 2, 128]),
                    reads=[B_omlD], writes=[B_oml])
                for gi, c0 in enumerate((h * 128, 1024 + h * 128, 2048 + h * 128, 3072 + h * 128, 4096 + h * 128)):
                    dma("sp", wh[:, :, gi, :], winb_d[:, c0:c0 + 128].rearrange("(kt p) n -> p kt n", p=128), reads=[B_winb[c0 // 512]], writes=[B_wh])

                def lat_cols(kt, p0, n):
                    if not colmaj:
                        return uT[:, kt, p0:p0 + n]
                    return uT[:, kt, :].rearrange("p (r c) -> p c r", c=64)[:, p0 // 32:(p0 + n) // 32, :]

                for ti in range(18):
                    pb = banks[ti % 2]
                    for kt in range(16):
                        if ti >= 2 and colmaj:
                            ucm = uT[:, kt, :].rearrange("p (r c) -> p c r", c=64)
                            for j4 in range(4):
                                mm(pb[32 * j4:32 * j4 + 32, 0:384], ucm[:, (ti - 2) * 4 + j4, :], wh[:, kt, 1:4, :], kt == 0, kt == 15,
                                   [B_uT, B_wh], [B_bank[ti % 2]], tp=(0, 32 * j4))
                        else:
                            lhsT = ucT[:, kt, ti * 128:(ti + 1) * 128] if ti < 2 else uT[:, kt, (ti - 2) * 128:(ti - 1) * 128]
                            mm(pb[:, 0:384], lhsT, wh[:, kt, 1:4, :], kt == 0, kt == 15, [B_uT, B_wh], [B_bank[ti % 2]])
                    e_ = etmp[ti % 2]
                    act(e_, pb[:, 0:256], AF.Sigmoid, [B_bank[ti % 2]], [B_et[ti % 2]], scale=-1.0)
                    tt("dve", ktok[:, ti, :, :], e_.rearrange("p (d c) -> p d c", d=2), omlrow, ALU.mult,
                       [B_et[ti % 2], B_oml], [B_ktok])
                    cp("act", vtok[:, ti, :], pb[:, 256:384], [B_bank[ti % 2]], [B_vtok])
                act(lftok_f, ktok_f, AF.Ln, [B_ktok], [B_lf], scale=-1.0, bias=1.0)
                for gi, (dst, B_dst) in ((0, (qT, B_qT)), (4, (gT, B_gT))):
                    for tc in range(4):
                        pb = banks[2 + tc % 2]
                        for kt in range(16):
                            mm(pb[:, :], wh[:, kt, gi, :], uT[:, kt, tc * 512:(tc + 1) * 512], kt == 0, kt == 15, [B_uT, B_wh], [B_bank[2 + tc % 2]])
                        cp("act" if tc % 2 == 0 else "dve", dst[:, tc * 512:(tc + 1) * 512], pb[:, :], [B_bank[2 + tc % 2]], [B_dst])
                act(gT, gT, AF.Silu, [B_gT], [B_gT])
                memset("pool", osum, 0.0, [B_osum])
                memset("pool", Sst[:], 0.0, [B_S[0], B_S[1]])
                memset("pool", Sbf[:], 0.0, [B_Sb[0], B_Sb[1]])
                order = {0: list(range(18)), 1: [1, 0] + list(range(17, 1, -1))}

                def qslice(p0):
                    if not colmaj:
                        return qT[:, p0:p0 + 128]
                    return qT.rearrange("p (r c) -> p c r", c=64)[:, p0 // 32:p0 // 32 + 4, :]

                def stage1a(step, dr):
                    ti = order[dr][step]
                    is_ctx = ti < 2
                    st_ = 3 * dr + step % 3
                    bA, bC = banks[2 + 3 * dr], banks[4 + 3 * dr]
                    BA = B_bank[2 + 3 * dr]
                    lf = lftok[:, ti, dr, :]
                    kk = ktok[:, ti, dr, :]
                    E = Eb[st_]
                    pA = bA[:].rearrange("p (s c) -> p s c", s=4)
                    cp("pool", lfh[st_], lf, [B_lf], [B_lfh[st_]])
                    tt("pool", lfl[st_], lf, lfh[st_], ALU.subtract, [B_lf, B_lfh[st_]], [B_lfh[st_]])
                    Rd = [B_lfh[st_], B_cmb]

                    def acc2(o_, c_idx, feat):
                        for hl, lx in enumerate((lfh[st_], lfl[st_])):
                            if feat:
                                mm(o_, lx, cmb[:, 4 * dr + c_idx, :], hl == 0, hl == 1, Rd, [BA])
                            else:
                                mm(o_, cmb[:, 4 * dr + c_idx, :], lx, hl == 0, hl == 1, Rd, [BA])
                    if not is_ctx:
                        acc2(pA[:, 0, :], 0, True)
                        acc2(pA[:, 2, :], 2, False)
                    acc2(pA[:, 1, :], 1, True)
                    acc2(pA[:, 3, :], 3, False)
                    if is_ctx:
                        act(E[:, 1, :], pA[:, 1, :], AF.Exp, [BA], [B_E[st_]])
                        act(E[:, 3, :], pA[:, 3, :], AF.Exp, [BA], [B_E[st_]])
                    else:
                        act(E, pA, AF.Exp, [BA], [B_E[st_]])
                    tt("dve", Khat[st_], kk, E[:, 3, :], ALU.mult, [B_ktok, B_E[st_]], [B_Kh[st_]])
                    if not is_ctx:
                        p0 = (ti - 2) * 128
                        tt("dve", Ktil[st_], kk, E[:, 2, :], ALU.mult, [B_ktok, B_E[st_]], [B_Kt[st_]])
                        qs = qslice(p0)
                        if colmaj:
                            e0 = E[:, 0, :].rearrange("p (c r) -> p c r", r=32)
                            e2 = E[:, 1, :].rearrange("p (c r) -> p c r", r=32)
                            o0 = QtilT[st_].rearrange("p (c r) -> p c r", r=32)
                            o2 = QaT[st_].rearrange("p (c r) -> p c r", r=32)
                        else:
                            e0, e2, o0, o2 = E[:, 0, :], E[:, 1, :], QtilT[st_], QaT[st_]
                        tt("pool", o0, qs, e0, ALU.mult, [B_qT, B_E[st_]], [B_Qt[st_]])
                        tt("pool", o2, qs, e2, ALU.mult, [B_qT, B_E[st_]], [B_Qa[st_]])

                def stage1b(step, dr):
                    ti = order[dr][step]
                    if ti < 2:
                        return
                    st_ = 3 * dr + step % 3
                    bC = banks[4 + 3 * dr]
                    pT = bC[:, 256:384].bitcast(BF16)[:, 0:128]
                    tr(pT, Ktil[st_], identb[:], [B_Kt[st_], B_const], [B_ct[dr]])
                    cp("act", KtilT[st_], pT, [B_ct[dr]], [B_KtT[st_]])
                    mm(bC[:, 0:128], KtilT[st_], QtilT[st_], True, True, [B_KtT[st_], B_Qt[st_]], [B_cs[dr]])
                    tt("dve", scm[st_], bC[:, 0:128], cm[:, 0 + dr, :], ALU.mult, [B_cs[dr], B_const], [B_scm[st_]])

                def stage2(step, dr):
                    ti = order[dr][step]
                    is_ctx = ti < 2
                    st_ = 3 * dr + step % 3
                    bO, bC = banks[3 + 3 * dr], banks[4 + 3 * dr]
                    BO = B_bank[3 + 3 * dr]
                    E = Eb[st_]
                    vv = vtok[:, ti, :]
                    if not is_ctx:
                        mm(bO[:, 0:128], vv, scm[st_], True, False, [B_vtok, B_scm[st_]], [BO])
                    chunks = (0, 1) if dr == 0 else (1, 0)
                    for ci, c in enumerate(chunks):
                        cs = slice(c * 64, (c + 1) * 64)
                        if not is_ctx:
                            mm(bO[:, cs], Sbf[:, dr, :], QaT[st_][:, cs], False, ci == 1, [B_Sb[dr], B_Qa[st_]], [BO])
                        mm(bC[:, 128:256], Khat[st_][cs, :], vtok[cs, ti, :], True, True, [B_Kh[st_], B_vtok], [B_cu[dr]])
                        dcol = (c * 64 + 63) if dr == 0 else (c * 64)
                        stt("dve", Sst[:, dr, :], Sst[:, dr, :], E[:, 1, dcol:dcol + 1], bC[:, 128:256], ALU.mult, ALU.add,
                            [B_S[dr], B_E[st_], B_cu[dr]], [B_S[dr]])
                        cp("act", Sbf[:, dr, :], Sst[:, dr, :], [B_S[dr]], [B_Sb[dr]])
                    if not is_ctx:
                        p0 = (ti - 2) * 128
                        tt("dve", osum[:, p0:p0 + 128], osum[:, p0:p0 + 128], bO[:, 0:128], ALU.add, [B_osum, BO], [B_osum])

                for it in range(20):
                    if it < 18:
                        for dr in range(2):
                            stage1a(it, dr)
                    if 1 <= it < 19:
                        for dr in range(2):
                            stage1b(it - 1, dr)
                    if it >= 2:
                        for dr in range(2):
                            stage2(it - 2, dr)
                act(qT, osum, AF.Square, [B_osum], [B_qT])
                for tc in range(4):
                    pb = banks[tc % 2]
                    cs = slice(tc * 512, (tc + 1) * 512)
                    mm(pb[:, :], ONES, qT[:, cs], True, True, [B_const, B_qT], [B_bank[tc % 2]])
                    act(rtmp, pb[:, :], AF.Sqrt, [B_bank[tc % 2]], [B_rtmp], scale=1.0 / 128, bias=EPS)
                    recip(rtmp, rtmp, [B_rtmp], [B_rtmp])
                    tt("dve", rtmp, rtmp, osum[:, cs], ALU.mult, [B_rtmp, B_osum], [B_rtmp])
                    if colmaj:
                        o_ = astage.rearrange("p (r c) -> p c r", c=64)[:, tc * 16:(tc + 1) * 16, :]
                        i0 = rtmp.rearrange("p (c r) -> p c r", r=32)
                        i1 = gT.rearrange("p (r c) -> p c r", c=64)[:, tc * 16:(tc + 1) * 16, :]
                    else:
                        o_, i0, i1 = astage[:, cs], rtmp, gT[:, cs]
                    stt("dve", o_, i0, vT[:, R_HG + h:R_HG + h + 1], i1, ALU.mult, ALU.mult, [B_rtmp, B_vT, B_gT], [B_ast])
                dma("pool", mix_d[b, h * 128:(h + 1) * 128, :], astage, reads=[B_ast], writes=[B_mixD])
                issue_precast(4)

            for hf in range(2):
                S.phase_fence()
                a2 = Alloc(arena2)
                uuT = a2([4, L], BF16)
                x0T = a2([4, L], BF16)
                uutok = a2([16, 512], BF16)
                mark = a2.off
                whb = [a2([16, 3, 128], BF16) for _ in range(2)]
                pbuf = [a2([L + 2], F32) for _ in range(2)]
                ytm = [a2([L], F32) for _ in range(2)]
                vc = a2([L], F32)
                B_uuT = S.bufs("uuT", 4)
                B_x0T = S.bufs("x0T", 4)
                B_uutok = S.buf("uutok")
                B_whb = S.bufs("whb", 2)
                B_pb = S.bufs("pbuf", 2)
                B_ytm = S.bufs("ytm", 2)
                B_vc = S.buf("vc")
                for k in range(2):
                    memset("pool", pbuf[k][:, 0:1], 0.0, [B_pb[k]])
                    memset("pool", pbuf[k][:, L + 1:L + 2], 0.0, [B_pb[k]])
                si = 0
                for ct in range(4):
                    cg = hf * 4 + ct
                    k = ct % 2
                    for s_ in range(3):
                        c0 = 5120 + s_ * 1024 + cg * 128
                        dma("sp", whb[k][:, :, s_, :], winb_d[:, c0:c0 + 128].rearrange("(kt p) n -> p kt n", p=128), reads=[B_winb[c0 // 512]], writes=[B_whb[k]])
                    for s_ in range(3):
                        pk = si % 2
                        si += 1
                        for tc in range(4):
                            pb = banks[tc % 2]
                            for kt in range(16):
                                mm(pb[:, :], whb[k][:, kt, s_, :], uT[:, kt, tc * 512:(tc + 1) * 512], kt == 0, kt == 15, [B_uT, B_whb[k]], [B_bank[tc % 2]])
                            cp("act", pbuf[pk][:, 1 + tc * 512:1 + (tc + 1) * 512], pb[:, :], [B_bank[tc % 2]], [B_pb[pk]])
                        ch = s_ * 8 + cg
                        w0 = vT[:, R_CW + 0 * 24 + ch:R_CW + 0 * 24 + ch + 1]
                        w1_ = vT[:, R_CW + 1 * 24 + ch:R_CW + 1 * 24 + ch + 1]
                        w2_ = vT[:, R_CW + 2 * 24 + ch:R_CW + 2 * 24 + ch + 1]
                        cb = vT[:, R_CB + ch:R_CB + ch + 1]
                        y_ = vc if s_ == 0 else ytm[pk]
                        B_y = B_vc if s_ == 0 else B_ytm[pk]
                        act(y_, pbuf[pk][:, 1:L + 1], AF.Identity, [B_pb[pk], B_vT], [B_y], scale=w1_, bias=cb)
                        stt("dve", y_, pbuf[pk][:, 0:L], w0, y_, ALU.mult, ALU.add, [B_pb[pk], B_vT, B_y], [B_y])
                        stt("dve", y_, pbuf[pk][:, 2:L + 2], w2_, y_, ALU.mult, ALU.add, [B_pb[pk], B_vT, B_y], [B_y])
                        if s_ == 1:
                            tt("dve", uuT[:, ct, :], y_, vc, ALU.mult, [B_y, B_vc], [B_uuT[ct]])
                        elif s_ == 2:
                            cp("pool", x0T[:, ct, :], y_, [B_y], [B_x0T[ct]])
                    for q4 in range(4):
                        bi = 2 + q4 % 2
                        pbf = banks[bi][:].bitcast(BF16)
                        for q in range(4):
                            s16 = q4 * 4 + q
                            tr(pbf[:, q * 128:(q + 1) * 128], uuT[:, ct, s16 * 128:(s16 + 1) * 128], identb[:], [B_uuT[ct], B_const], [B_bank[bi]])
                        cp("act" if q4 % 2 == 0 else "dve", uutok[:, q4 * 4:(q4 + 1) * 4, ct * 128:(ct + 1) * 128],
                           pbf[:, 0:512].rearrange("p (q c) -> p q c", q=4), [B_bank[bi]], [B_uutok])
                S.phase_fence()
                a2.off = mark
                Y = a2([32, 512], BF16)
                mark2 = a2.off
                ftb = [a2([16, 2, 128], BF16) for _ in range(2)]
                kb = [a2([2, 512], F32) for _ in range(2)]
                tq = [a2([512], F32) for _ in range(4)]
                B_Y = S.buf("Y")
                B_ftb = S.bufs("ftb", 2)
                B_kb = S.bufs("kb", 2)
                B_tq = S.bufs("tq", 4)
                for T in range(16):
                    k = T % 2
                    dma("sp", ftb[k], FTt_d[T], writes=[B_ftb[k]])
                    dma("sp", kb[k], ksp_d[T][:, :, hf * 512:(hf + 1) * 512], reads=[B_kspD], writes=[B_kb[k]])
                    pr, pi_ = banks[2 * k], banks[2 * k + 1]
                    Br, Bi = B_bank[2 * k], B_bank[2 * k + 1]
                    for ri, (pp, Bp) in enumerate(((pr, Br), (pi_, Bi))):
                        for s_ in range(16):
                            mm(pp[:, :], ftb[k][:, s_, ri, :], uutok[:, s_, :], s_ == 0, s_ == 15, [B_ftb[k], B_uutok], [Bp])
                    tt("dve", tq[0], pr[:, :], kb[k][:, 0, :], ALU.mult, [Br, B_kb[k]], [B_tq[0]])
                    tt("dve", tq[1], pi_[:, :], kb[k][:, 1, :], ALU.mult, [Bi, B_kb[k]], [B_tq[1]])
                    tt("dve", tq[2], pr[:, :], kb[k][:, 1, :], ALU.mult, [Br, B_kb[k]], [B_tq[2]])
                    tt("dve", tq[3], pi_[:, :], kb[k][:, 0, :], ALU.mult, [Bi, B_kb[k]], [B_tq[3]])
                    tt("pool", Y[:, T, :], tq[0], tq[1], ALU.subtract, [B_tq[0], B_tq[1]], [B_Y])
                    tt("pool", Y[:, 16 + T, :], tq[2], tq[3], ALU.add, [B_tq[2], B_tq[3]], [B_Y])
                    if T == 0:
                        cp("act", Y[0:1, 0, :], tq[0][0:1, :], [B_tq[0], B_Y], [B_Y])
                        cp("act", Y[0:1, 16, :], tq[1][0:1, :], [B_tq[1], B_Y], [B_Y])
                S.phase_fence()
                a2.off = mark2
                gtb = [a2([32, 256], BF16) for _ in range(2)]
                et = [a2([256], F32) for _ in range(2)]
                B_gtb = S.bufs("gtb", 2)
                B_et2 = S.bufs("et2", 2)
                for tcn in range(8):
                    k = tcn % 2
                    dma("sp", gtb[k], GTt_d[tcn], writes=[B_gtb[k]])
                    for ct in range(4):
                        cg = hf * 4 + ct
                        bi = (tcn * 4 + ct) % 4
                        pb = banks[bi]
                        for gt in range(32):
                            mm(pb[:, 0:256], Y[:, gt, ct * 128:(ct + 1) * 128], gtb[k][:, gt, :], gt == 0, gt == 31, [B_Y, B_gtb[k]], [B_bank[bi]])
                        e2 = et[ct % 2]
                        tsl = slice(tcn * 256, (tcn + 1) * 256)
                        stt("dve", e2, uuT[:, ct, tsl], vT[:, R_HB + cg:R_HB + cg + 1], pb[:, 0:256], ALU.mult, ALU.add,
                            [B_uuT[ct], B_vT, B_bank[bi]], [B_et2[ct % 2]])
                        tt("pool", uuT[:, ct, tsl], e2, x0T[:, ct, tsl], ALU.mult, [B_et2[ct % 2], B_x0T[ct]], [B_uuT[ct]])
                for ct in range(4):
                    cg = hf * 4 + ct
                    dma("pool", mix_d[b, 1024 + cg * 128:1024 + (cg + 1) * 128, :], uuT[:, ct, :], reads=[B_uuT[ct]], writes=[B_mixD])

            issue_precast(len(pending))
            S.phase_fence()
            a1 = Alloc(arena1)
            hT = a1([64, 512], BF16)
            fnrow = a1([D], F32)
            a2 = Alloc(arena2)
            x1 = a2([4, D], F32)
            mcu = a2([16, 512], BF16)
            NWO, NW1, NW2 = 2, 3, 3
            wo = [a2([16, 256], BF16) for _ in range(NWO)]
            w1c = [a2([16, 256], BF16) for _ in range(NW1)]
            w2c = [a2([4, 512], BF16) for _ in range(NW2)]
            xn2 = a2([D], BF16)
            gbuf = a2([2, D], F32)
            tmpd = [a2([512], F32) for _ in range(2)]
            junk = xn2
            B_hT = S.bufs("hT", 64)
            B_fn = S.buf("fnrow")
            B_x1 = S.bufs("x1_", 4)
            B_mcu = S.buf("mcu")
            B_wo = S.bufs("wo", NWO)
            B_w1c = S.bufs("w1c", NW1)
            B_w2c = S.bufs("w2c", NW2)
            B_xn2 = S.buf("xn2")
            B_g = S.buf("gbuf")
            B_tmpd = S.bufs("tmpd", 2)
            B_junk = B_xn2
            dma("sp", fnrow, fng_d[0:1, :].broadcast_to([128, D]), writes=[B_fn])
            dma("sp", gbuf[:, 0, :], mrow_d[b:b + 1, 32 * 128:48 * 128].broadcast_to([128, D]), reads=[B_mrowD], writes=[B_g])
            dma("sp", gbuf[:, 1, :], mrow_d[b:b + 1, 80 * 128:96 * 128].broadcast_to([128, D]), reads=[B_mrowD], writes=[B_g])
            tcount = 0
            for G in range(4):
                t0 = b * L + G * 512
                dma("sp", mcu, mix_d[b, :, G * 512:(G + 1) * 512].rearrange("(ft p) t -> p ft t", p=128), reads=[B_mixD], writes=[B_mcu])
                for tq_ in range(4):
                    dma("sp", x1[:, tq_, :], x_d[t0 + tq_ * 128:t0 + (tq_ + 1) * 128, :], writes=[B_x1[tq_]])
                for dc in range(8):
                    k = (G * 8 + dc) % NWO
                    dsl = slice(dc * 256, (dc + 1) * 256)
                    dma("sp", wo[k], woutb_d[:, dsl].rearrange("(ft p) n -> p ft n", p=128), reads=[B_woutb[dc // 2]], writes=[B_wo[k]])
                    for tq_ in range(4):
                        bi = (dc * 4 + tq_) % 4
                        pb = banks[bi]
                        for ft in range(16):
                            mm(pb[:, 0:256], mcu[:, ft, tq_ * 128:(tq_ + 1) * 128], wo[k][:, ft, :], ft == 0, ft == 15, [B_mcu, B_wo[k]], [B_bank[bi]])
                        td = tmpd[tcount % 2]
                        Btd = B_tmpd[tcount % 2]
                        tcount += 1
                        tt("dve", td[:, 0:256], pb[:, 0:256], gbuf[:, 0, dsl], ALU.mult, [B_bank[bi], B_g], [Btd])
                        tt("pool", x1[:, tq_, dsl], x1[:, tq_, dsl], td[:, 0:256], ALU.add, [B_x1[tq_], Btd], [B_x1[tq_]])
                for tq_ in range(4):
                    sm = small[:, 16 + 4 * (tq_ % 2): 20 + 4 * (tq_ % 2)]
                    memset("dve", sm[:, 0:1], 0.0, [B_small])
                    act(junk, x1[:, tq_, :], AF.Square, [B_x1[tq_]], [B_junk, B_small], accum=sm[:, 0:1])
                    rstd_from_ssq(sm[:, 0:1], sm[:, 1:2], sm[:, 2:3], D, B_small)
                    ts("dve", xn2, x1[:, tq_, :], sm[:, 2:3], None, ALU.mult, None, [B_x1[tq_], B_small], [B_xn2])
                    for half in range(2):
                        bi = 4 + (tq_ * 2 + half) % 4
                        pbf = banks[bi][:].bitcast(BF16)
                        for q in range(8):
                            dt = half * 8 + q
                            tr(pbf[:, q * 128:(q + 1) * 128], xn2[:, dt * 128:(dt + 1) * 128], identb[:], [B_xn2, B_const], [B_bank[bi]])
                        for q in range(8):
                            dt = half * 8 + q
                            o_ = mcu[:, dt, tq_ * 128:(tq_ + 1) * 128]
                            i_ = pbf[:, q * 128:(q + 1) * 128]
                            if q % 2 == 0:
                                act(o_, i_, AF.Identity, [B_bank[bi], B_scp, B_mT], [B_mcu], scale=sc2p[:, dt, b:b + 1], bias=mT[:, 48 + dt, b:b + 1])
                            else:
                                ts("dve", o_, i_, sc2p[:, dt, b:b + 1], mT[:, 48 + dt, b:b + 1], ALU.mult, ALU.add, [B_bank[bi], B_scp, B_mT], [B_mcu])
                for fc in range(32):
                    k = fc % NW1
                    dma("sp", w1c[k], w1b_d[:, fc * 256:(fc + 1) * 256].rearrange("(kt p) n -> p kt n", p=128), reads=[B_w1b[fc // 2]], writes=[B_w1c[k]])
                    for f2 in range(2):
                        fft = fc * 2 + f2
                        bi = fft % 4
                        pb = banks[bi]
                        for kt in range(16):
                            mm(pb[:, :], w1c[k][:, kt, f2 * 128:(f2 + 1) * 128], mcu[:, kt, :], kt == 0, kt == 15, [B_w1c[k], B_mcu], [B_bank[bi]])
                        td = tmpd[tcount % 2]
                        Btd = B_tmpd[tcount % 2]
                        tcount += 1
                        act(td, pb[:, :], AF.Relu, [B_bank[bi]], [Btd])
                        tt("dve" if fft % 2 == 0 else "pool", hT[:, fft, :], td, td, ALU.mult, [Btd], [B_hT[fft]])
                for dc in range(4):
                    dsl = slice(dc * 512, (dc + 1) * 512)
                    pbs = [banks[(dc % 2) * 4 + tq_] for tq_ in range(4)]
                    Bps = [B_bank[(dc % 2) * 4 + tq_] for tq_ in range(4)]
                    for f4 in range(16):
                        k = f4 % NW2
                        dma("sp", w2c[k], w2b_d[f4 * 512:(f4 + 1) * 512, dsl].rearrange("(f p) n -> p f n", p=128), reads=[B_w2b[f4 // 2]], writes=[B_w2c[k]])
                        for f_ in range(4):
                            fft = f4 * 4 + f_
                            for tq_ in range(4):
                                mm(pbs[tq_][:, :], hT[:, fft, tq_ * 128:(tq_ + 1) * 128], w2c[k][:, f_, :], fft == 0, fft == 63,
                                   [B_hT[fft], B_w2c[k]], [Bps[tq_]])
                    for tq_ in range(4):
                        td = tmpd[tcount % 2]
                        Btd = B_tmpd[tcount % 2]
                        tcount += 1
                        tt("dve", td, pbs[tq_][:, :], gbuf[:, 1, dsl], ALU.mult, [Bps[tq_], B_g], [Btd])
                        tt("pool", x1[:, tq_, dsl], x1[:, tq_, dsl], td, ALU.add, [B_x1[tq_], Btd], [B_x1[tq_]])
                for tq_ in range(4):
                    sm = small[:, 24 + 4 * (tq_ % 2): 28 + 4 * (tq_ % 2)]
                    memset("dve", sm[:, 0:1], 0.0, [B_small])
                    act(junk, x1[:, tq_, :], AF.Square, [B_x1[tq_]], [B_junk, B_small], accum=sm[:, 0:1])
                    rstd_from_ssq(sm[:, 0:1], sm[:, 1:2], sm[:, 2:3], D, B_small)
                    stt("dve", x1[:, tq_, :], x1[:, tq_, :], sm[:, 2:3], fnrow, ALU.mult, ALU.mult, [B_x1[tq_], B_small, B_fn], [B_x1[tq_]])
                    dma("pool", out_d[t0 + tq_ * 128:t0 + (tq_ + 1) * 128, :], x1[:, tq_, :], reads=[B_x1[tq_]], final=True)

        S.emit(st)
        build_program.stats = S.stats
    return nc


_PROG = {}


def _layout_inputs(inp):
    f32 = lambda a: np.ascontiguousarray(np.asarray(a, dtype=np.float32))
    c = _consts()
    x = f32(inp["x"])
    ctx = f32(inp["ctx"])
    cvec = f32(inp["c"])
    cctx = f32(inp["c_ctx"])
    shared = dict(
        w_ada=f32(inp["w_ada"])[0], b_ada=f32(inp["b_ada"])[0].reshape(1, -1),
        lbl=f32(inp["hgrn_lb_logits"]).reshape(1, 4096),
        w_in=f32(inp["w_in"])[0], w_out=f32(inp["w_out"])[0], w_mlp1=f32(inp["w_mlp1"])[0], w_mlp2=f32(inp["w_mlp2"])[0],
        flt_w1=f32(inp["flt_w1"])[0], flt_w2=f32(inp["flt_w2"])[0], flt_w3=f32(inp["flt_w3"])[0],
        fng=f32(inp["final_norm_g"]).reshape(1, -1),
        identb=c["identb"], identf=c["identf"], cmats=c["cmats"], zT=c["zT"], deltas=c["deltas"], negt=c["negt"],
        FTt=c["FTt"], GTt=c["GTt"],
    )
    rows = np.zeros((NROWS, 128), np.float32)
    rows[R_BADA:R_BADA + 96] = f32(inp["b_ada"])[0].reshape(96, 128)
    rows[R_N1:R_N1 + 16] = f32(inp["norm1_g"])[0].reshape(16, 128)
    rows[R_LB:R_LB + 32] = f32(inp["hgrn_lb_logits"]).reshape(32, 128)
    rows[R_HG:R_HG + 8] = f32(inp["hgrn_norm_g"])[0].reshape(8, 128)
    rows[R_CW:R_CW + 72] = f32(inp["hy_conv_w"])[0].reshape(72, 128)
    rows[R_CB:R_CB + 24] = f32(inp["hy_conv_b"])[0].reshape(24, 128)
    rows[R_HB:R_HB + 8] = f32(inp["hy_bias"])[0].reshape(8, 128)
    rows[R_N2:R_N2 + 16] = f32(inp["norm2_g"])[0].reshape(16, 128)
    rows[R_FN:R_FN + 16] = f32(inp["final_norm_g"]).reshape(16, 128)
    rows[R_FLT + 0, 0:64] = f32(inp["flt_b1"])[0]
    rows[R_FLT + 1, 0:64] = f32(inp["flt_freq"])[0]
    rows[R_FLT + 2, 0:64] = f32(inp["flt_b2"])[0]
    maps = []
    for core in range(8):
        r = rows.copy()
        for j in range(2):
            r[R_C + j * 16:R_C + (j + 1) * 16] = cvec[core * NB + j].reshape(16, 128)
        r[R_C + 32:R_C + 48] = cctx.reshape(16, 128)
        m = dict(shared)
        m["x"] = x[core * NB:(core + 1) * NB].reshape(NB * L, D)
        m["ctx"] = ctx[core * NB:(core + 1) * NB].reshape(NB * CL, D)
        m["vecs"] = r
        maps.append(m)
    return maps


def kernel(**inputs):
    if "nc" not in _PROG:
        _PROG["nc"] = build_program(DEBUG)
    nc = _PROG["nc"]
    maps = _layout_inputs(inputs)
    res = run_bass_kernel_spmd(nc, maps, core_ids=list(range(8)))
    if DEBUG:
        kernel.last = res
    out = np.concatenate([np.asarray(r["out"]).reshape(NB, L, D) for r in res.results], axis=0)
    return out.astype(np.float32)
```

```python
import os
from contextlib import ExitStack
import numpy as np
import ml_dtypes
import concourse.bass as bass
import concourse.mybir as mybir
from concourse.bass_utils import run_bass_kernel_spmd

F32 = mybir.dt.float32
BF16 = mybir.dt.bfloat16
AF = mybir.ActivationFunctionType
ALU = mybir.AluOpType

D = 2048
L = 2048
CL = 256
NB = 2
EPS = 1e-6
PI = float(np.pi)
DEBUG = bool(int(os.environ.get("MK_DEBUG", "0")))

R_C, R_BADA, R_N1, R_LB, R_HG, R_CW, R_CB, R_HB, R_N2, R_FN, R_FLT = 0, 48, 144, 160, 192, 200, 272, 296, 304, 320, 336
NROWS = 384

ENGS = ("pe", "act", "dve", "pool", "sp")
EPOCH = 30000
NSEM_ENG = 4
NDMA_SEM = 12


class Buf:
    __slots__ = ("name", "last_writer", "readers")

    def __init__(self, name):
        self.name = name
        self.last_writer = None
        self.readers = []


class Op:
    __slots__ = ("idx", "eng", "fn", "deps", "is_dma", "has_dep", "sem", "val")

    def __init__(self, idx, eng, fn, is_dma):
        self.idx = idx
        self.eng = eng
        self.fn = fn
        self.deps = set()
        self.is_dma = is_dma
        self.has_dep = False
        self.sem = None
        self.val = None


class Sched:
    def __init__(self, nc):
        self.nc = nc
        self.ops = []
        self.final_waits = []
        self.last_eng = {}
        self.dma_slots = {}
        self.dma_cnt = {"sp": 0, "pool": 0, "act": 0}
        self.fence = set()

    def buf(self, name):
        return Buf(name)

    def bufs(self, name, n):
        return [Buf(f"{name}{i}") for i in range(n)]

    def phase_fence(self):
        self.fence = set(self.last_eng.values()) | set(self.dma_slots.values())

    def op(self, eng, fn, reads=(), writes=(), is_dma=False, final=False):
        o = Op(len(self.ops), eng, fn, is_dma)
        for b in reads:
            if b.last_writer is not None:
                o.deps.add(b.last_writer)
        for b in writes:
            if b.last_writer is not None:
                o.deps.add(b.last_writer)
            for r in b.readers:
                o.deps.add(r)
        for b in reads:
            if not is_dma:
                b.readers = [r for r in b.readers if r != o.idx and (self.ops[r].is_dma or self.ops[r].eng != eng)]
            b.readers.append(o.idx)
        for b in writes:
            b.last_writer = o.idx
            b.readers = []
        o.deps |= self.fence
        if is_dma:
            slot = (eng, self.dma_cnt[eng] % NDMA_SEM)
            self.dma_cnt[eng] += 1
            if slot in self.dma_slots:
                o.deps.add(self.dma_slots[slot])
            self.dma_slots[slot] = o.idx
        else:
            self.last_eng[eng] = o.idx
        o.deps.discard(o.idx)
        self.ops.append(o)
        if final:
            self.final_waits.append(o.idx)
        return o

    def emit(self, stack):
        nc = self.nc
        ops = self.ops
        for o in ops:
            if o.eng == "pe" and not o.is_dma:
                o.deps = {d for d in o.deps if not (ops[d].eng == "pe" and not ops[d].is_dma)}
            for d in o.deps:
                ops[d].has_dep = True
        for i in self.final_waits:
            ops[i].has_dep = True
        tick_sems = {e: [stack.enter_context(nc.semaphore(f"t_{e}{k}")) for k in range(NSEM_ENG)] for e in ENGS}
        dma_sems = {e: [stack.enter_context(nc.semaphore(f"d_{e}{k}")) for k in range(NDMA_SEM)] for e in ("sp", "pool")}
        tick = {e: 0 for e in ENGS}
        dcnt = {e: 0 for e in dma_sems}
        dval = {}
        for o in ops:
            if o.is_dma:
                j = dcnt[o.eng] % NDMA_SEM
                dcnt[o.eng] += 1
                key = (o.eng, j)
                dval[key] = dval.get(key, 0) + 16
                o.sem = dma_sems[o.eng][j]
                o.val = dval[key]
                o.has_dep = True
            elif o.has_dep:
                t = tick[o.eng]
                tick[o.eng] += 1
                o.sem = tick_sems[o.eng][t // EPOCH]
                o.val = t % EPOCH + 1
        self.stats = dict(n_ops=len(ops), ticks=dict(tick), dmas=dict(dcnt))
        per_eng = {e: [o for o in ops if o.eng == e] for e in ENGS}
        final = [ops[i] for i in self.final_waits]

        def run(eng_name, eng):
            waited = {}

            def wait_for(p):
                k = id(p.sem)
                if waited.get(k, 0) >= p.val:
                    return
                eng.wait_ge(p.sem, p.val)
                waited[k] = p.val

            for o in per_eng[eng_name]:
                for d in sorted(o.deps):
                    wait_for(ops[d])
                ins = o.fn(eng)
                if o.has_dep:
                    ins.then_inc(o.sem, 16 if o.is_dma else 1)
            if eng_name == "sp":
                for p in final:
                    wait_for(p)

        with nc.Block() as block:
            @block.tensor
            def _(e):
                run("pe", e)

            @block.scalar
            def _(e):
                run("act", e)

            @block.vector
            def _(e):
                run("dve", e)

            @block.gpsimd
            def _(e):
                run("pool", e)

            @block.sync
            def _(e):
                run("sp", e)


_CONST = {}


def _consts():
    if _CONST:
        return _CONST
    bf = ml_dtypes.bfloat16
    c = {}
    c["identb"] = np.eye(128, dtype=np.float32).astype(bf)
    c["identf"] = np.eye(128, dtype=np.float32)
    j = np.arange(128)
    same = (j[:, None] // 64) == (j[None, :] // 64)
    lj = (j % 64)[:, None]
    lx = (j % 64)[None, :]
    A_f = same & (lj <= lx)
    A_b = same & (lj >= lx)
    Mid_f = same & (lj <= 31)
    Mid_b = same & (lj >= 32)
    End = same
    f = lambda m: m.astype(np.float32)
    mats = [f(A_f), f(A_b), f(A_f) - f(Mid_f), f(A_b) - f(Mid_b), f(Mid_f) - f(A_f), f(Mid_b) - f(A_b),
            f(End) - f(A_f), f(End) - f(A_b), np.ones((128, 128), np.float32)]
    c["cmats"] = np.ascontiguousarray(np.stack(mats, axis=1))
    pos = np.arange(L, dtype=np.float32)[:, None]
    t = pos / np.float32(max(L - 1, 1))
    bands = np.linspace(1e-4, 16 - 1, 16, dtype=np.float32)[None, :]
    ang = bands * np.float32(2.0 * np.pi) * pos / np.float32(L)
    z = np.concatenate([t, np.cos(ang), -np.sin(ang)], axis=-1).astype(np.float32)
    c["zT"] = np.ascontiguousarray(z.T)
    deltas = np.abs(np.linspace(np.log(1e-2) / 1.5, np.log(1e-2) / 0.3, 1024, dtype=np.float32))
    deltas = np.tile(deltas, 2).astype(np.float32)
    c["deltas"] = np.ascontiguousarray(np.broadcast_to(deltas[None, :], (128, 2048)))
    c["negt"] = np.ascontiguousarray((-t[:, 0]).reshape(16, 128).T)
    s = np.arange(2048, dtype=np.float64)[:, None]
    g = np.arange(2048, dtype=np.float64)[None, :]
    th = 2.0 * np.pi * ((s * g) % 4096) / 4096.0
    C = np.cos(th)
    Sn = np.sin(th)
    FT = np.empty((2048, 4096), np.float64)
    FT[:, :2048] = C
    FT[:, 2048:] = -Sn
    FT[:, 2048] = (-1.0) ** np.arange(2048)
    FTt = FT.reshape(16, 128, 2, 16, 128).transpose(3, 1, 0, 2, 4)
    c["FTt"] = np.ascontiguousarray(FTt).astype(bf)
    GT = np.empty((4096, 2048), np.float64)
    GT[:2048] = C * (2.0 / 4096.0)
    GT[2048:] = -Sn * (2.0 / 4096.0)
    GT[0] = 1.0 / 4096.0
    GT[2048] = ((-1.0) ** np.arange(2048)) / 4096.0
    GTt = GT.reshape(32, 128, 8, 256).transpose(2, 1, 0, 3)
    c["GTt"] = np.ascontiguousarray(GTt).astype(bf)
    _CONST.update(c)
    return _CONST


def build_program(debug=False):
    nc = bass.Bass("TRN2", target_bir_lowering=False)
    din = lambda n, s, d=F32: nc.dram_tensor(n, list(s), d, kind="ExternalInput").ap()
    dscr = lambda n, s, d=F32: nc.dram_tensor(n, list(s), d, kind=("ExternalOutput" if debug else "Internal")).ap()
    x_d = din("x", [NB * L, D])
    ctx_d = din("ctx", [NB * CL, D])
    vecs_d = din("vecs", [NROWS, 128])
    wada_d = din("w_ada", [D, 6 * D])
    bada_d = din("b_ada", [1, 6 * D])
    lbl_d = din("lbl", [1, 4096])
    win_d = din("w_in", [D, 8192])
    wout_d = din("w_out", [D, D])
    w1_d = din("w_mlp1", [D, 8192])
    w2_d = din("w_mlp2", [8192, D])
    fw1_d = din("flt_w1", [33, 64])
    fw2_d = din("flt_w2", [64, 64])
    fw3_d = din("flt_w3", [64, 2048])
    fng_d = din("fng", [1, D])
    identb_d = din("identb", [128, 128], BF16)
    identf_d = din("identf", [128, 128])
    cmats_d = din("cmats", [128, 9, 128])
    zT_d = din("zT", [33, 2048])
    deltas_d = din("deltas", [128, 2048])
    negt_d = din("negt", [128, 16])
    FTt_d = din("FTt", [16, 128, 16, 2, 128], BF16)
    GTt_d = din("GTt", [8, 128, 32, 256], BF16)
    out_d = nc.dram_tensor("out", [NB * L, D], F32, kind="ExternalOutput").ap()
    mrow_d = dscr("mrow", [3, 6 * D])
    oml_d = dscr("omlD", [1, 2048])
    ksp_d = dscr("kspec", [16, 128, 2, 1024])
    mix_d = dscr("mixD", [NB, D, L], BF16)
    winb_d = nc.dram_tensor("winb", [D, 8192], BF16).ap()
    woutb_d = nc.dram_tensor("woutb", [D, D], BF16).ap()
    w1b_d = nc.dram_tensor("w1b", [D, 8192], BF16).ap()
    w2b_d = nc.dram_tensor("w2b", [8192, D], BF16).ap()

    st = ExitStack()
    with st:
        S = Sched(nc)
        sbt = lambda n, s, d: st.enter_context(nc.sbuf_tensor(n, list(s), d))
        A1_BYTES = 72 * 1024
        A2_BYTES = 124 * 1024
        arena1 = sbt("arena1", [128, A1_BYTES // 2], BF16)
        arena2 = sbt("arena2", [128, A2_BYTES // 2], BF16)
        identb = sbt("identb_s", [128, 128], BF16)
        identf = sbt("identf_s", [128, 128], F32)
        cm = sbt("cm_s", [128, 9, 128], F32)
        vT = sbt("vT", [128, NROWS], F32)
        mT = sbt("mT", [128, 96, 3], F32)
        sc1p = sbt("sc1p", [128, 16, 3], F32)
        sc2p = sbt("sc2p", [128, 16, 2], F32)
        Sst = sbt("Sst", [128, 2, 128], F32)
        Sbf = sbt("Sbf", [128, 2, 128], BF16)
        small = sbt("small", [128, 64], F32)
        banks = [st.enter_context(nc.psum_tensor(f"bank{i}", [128, 512], F32)) for i in range(8)]
        B_bank = S.bufs("bank", 8)
        B_const = S.buf("const")
        B_vT = S.buf("vT")
        B_mT = S.buf("mT")
        B_scp = S.buf("scp")
        B_small = S.buf("small")
        ONES = cm[:, 8, :]

        def carve(arena, off, shape, dtype, parts=128):
            esz = 2 if dtype == BF16 else 4
            n = int(np.prod(shape))
            assert off % 4 == 0
            lim = A1_BYTES if arena is arena1 else A2_BYTES
            assert off + n * esz <= lim, (off, n * esz, lim)
            a = arena[:, off // 2: off // 2 + n * esz // 2]
            if dtype != BF16:
                a = a.bitcast(dtype)
            if len(shape) == 2:
                a = a.rearrange("p (a b) -> p a b", a=shape[0])
            elif len(shape) == 3:
                a = a.rearrange("p (a b c) -> p a b c", a=shape[0], b=shape[1])
            elif len(shape) == 4:
                a = a.rearrange("p (a b c d) -> p a b c d", a=shape[0], b=shape[1], c=shape[2])
            if parts != 128:
                a = a[0:parts]
            return a

        class Alloc:
            def __init__(self, arena):
                self.arena = arena
                self.off = 0

            def __call__(self, shape, dtype, parts=128):
                esz = 2 if dtype == BF16 else 4
                a = carve(self.arena, self.off, shape, dtype, parts)
                self.off += (int(np.prod(shape)) * esz + 31) // 32 * 32
                return a

        def dma(eng, out, in_, reads=(), writes=(), final=False):
            S.op(eng, lambda e: e.dma_start(out=out, in_=in_), reads, writes, is_dma=True, final=final)

        def mm(out, lhsT, rhs, start, stop, reads, writes, tp=None):
            if tp is None:
                S.op("pe", lambda e: e.matmul(out, lhsT=lhsT, rhs=rhs, start=start, stop=stop), reads, writes)
            else:
                S.op("pe", lambda e: e.matmul(out, lhsT=lhsT, rhs=rhs, start=start, stop=stop, tile_position=tp), reads, writes)

        def tr(out, in_, ident, reads, writes):
            S.op("pe", lambda e: e.transpose(out, in_, ident), reads, writes)

        def act(out, in_, func, reads, writes, scale=1.0, bias=0.0, accum=None):
            if accum is None:
                S.op("act", lambda e: e.activation(out=out, in_=in_, func=func, bias=bias, scale=scale), reads, writes)
            else:
                S.op("act", lambda e: e.activation(out=out, in_=in_, func=func, bias=bias, scale=scale, accum_out=accum), reads, writes)

        def tt(eng, out, in0, in1, op, reads, writes):
            S.op(eng, lambda e: e.tensor_tensor(out=out, in0=in0, in1=in1, op=op), reads, writes)

        def ts(eng, out, in0, s1, s2, op0, op1, reads, writes):
            if s2 is None:
                S.op(eng, lambda e: e.tensor_scalar(out=out, in0=in0, scalar1=s1, scalar2=None, op0=op0), reads, writes)
            else:
                S.op(eng, lambda e: e.tensor_scalar(out=out, in0=in0, scalar1=s1, scalar2=s2, op0=op0, op1=op1), reads, writes)

        def stt(eng, out, in0, scalar, in1, op0, op1, reads, writes):
            S.op(eng, lambda e: e.scalar_tensor_tensor(out=out, in0=in0, scalar=scalar, in1=in1, op0=op0, op1=op1), reads, writes)

        def cp(eng, out, in_, reads, writes):
            if eng == "act":
                S.op("act", lambda e: e.copy(out=out, in_=in_), reads, writes)
            else:
                S.op(eng, lambda e: e.tensor_copy(out, in_), reads, writes)

        def recip(out, in_, reads, writes):
            S.op("dve", lambda e: e.reciprocal(out=out, in_=in_), reads, writes)

        def memset(eng, ap, val, writes):
            S.op(eng, lambda e: e.memset(ap, val), (), writes)

        def rstd_from_ssq(ssq, tmp, rstd, n, B_s):
            act(tmp, ssq, AF.Sqrt, [B_s], [B_s], scale=1.0 / n, bias=EPS)
            recip(rstd, tmp, [B_s], [B_s])

        dma("sp", identb[:], identb_d, writes=[B_const])
        dma("sp", identf[:], identf_d, writes=[B_const])
        dma("sp", cm[:], cmats_d, writes=[B_const])

        B_winb = S.bufs("winb", 16)
        B_woutb = S.bufs("woutb", 4)
        B_w1b = S.bufs("w1b", 16)
        B_w2b = S.bufs("w2b", 8)
        for ch in (0, 2, 4, 6, 8, 1, 3, 5, 7, 9, 10, 11, 12, 13, 14, 15):
            dma("pool", winb_d[:, ch * 512:(ch + 1) * 512], win_d[:, ch * 512:(ch + 1) * 512], writes=[B_winb[ch]])
        pending = []
        for ch in range(4):
            pending.append((woutb_d[:, ch * 512:(ch + 1) * 512], wout_d[:, ch * 512:(ch + 1) * 512], B_woutb[ch]))
        for ch in range(16):
            pending.append((w1b_d[:, ch * 512:(ch + 1) * 512], w1_d[:, ch * 512:(ch + 1) * 512], B_w1b[ch]))
        for ch in range(8):
            pending.append((w2b_d[ch * 1024:(ch + 1) * 1024, :], w2_d[ch * 1024:(ch + 1) * 1024, :], B_w2b[ch]))

        def issue_precast(n):
            for _ in range(n):
                if pending:
                    o_, i_, B_ = pending.pop(0)
                    dma("pool", o_, i_, writes=[B_])

        a2 = Alloc(arena2)
        vr = a2([3, 128], F32)
        B_vr = S.buf("vr")
        dma("sp", vr, vecs_d.rearrange("(t p) c -> p t c", p=128), writes=[B_vr])
        for t in range(3):
            tr(banks[0][:, t * 128:(t + 1) * 128], vr[:, t, :], identf[:], [B_vr, B_const], [B_bank[0]])
        cp("dve", vT[:], banks[0][:, 0:384], [B_bank[0]], [B_vT])
        siluT = a2([48], F32)
        B_silu = S.buf("silu")
        act(siluT, vT[:, R_C:R_C + 48], AF.Silu, [B_vT], [B_silu])
        siluT3 = siluT.rearrange("p (j k) -> p k j", k=16)

        wa = [a2([16, 512], F32) for _ in range(3)]
        B_wa = S.bufs("wa", 3)
        brow = [a2([512], F32, parts=3) for _ in range(2)]
        B_brow = S.bufs("brow", 2)
        mrs = [a2([512], F32, parts=3) for _ in range(2)]
        B_mrs = S.bufs("mrs", 2)
        B_mrowD = S.buf("mrowD")
        for cc in range(24):
            k = cc % 2
            kw = cc % 3
            cs = slice(cc * 512, (cc + 1) * 512)
            dma("sp", wa[kw], wada_d[:, cs].rearrange("(kt p) n -> p kt n", p=128), writes=[B_wa[kw]])
            dma("sp", brow[k], bada_d[0:1, cs].broadcast_to([3, 512]), writes=[B_brow[k]])
            pb = banks[cc % 2]
            for kt in range(16):
                mm(pb[0:3, :], siluT3[:, kt, :], wa[kw][:, kt, :], kt == 0, kt == 15, [B_silu, B_wa[kw]], [B_bank[cc % 2]])
            tt("dve", mrs[k], pb[0:3, :], brow[k], ALU.add, [B_bank[cc % 2], B_brow[k]], [B_mrs[k]])
            dma("sp", mrow_d[:, cs], mrs[k], reads=[B_mrs[k]], writes=[B_mrowD])
            pt = banks[2 + cc % 2]
            for q in range(4):
                tr(pt[:, q * 3:(q + 1) * 3], mrs[k][:, q * 128:(q + 1) * 128], identf[0:3, 0:3], [B_mrs[k], B_const], [B_bank[2 + cc % 2]])
            cp("act", mT[:, cc * 4:(cc + 1) * 4, :], pt[:, 0:12].rearrange("p (q j) -> p q j", j=3), [B_bank[2 + cc % 2]], [B_mT])
        for j in range(3):
            stt("dve", sc1p[:, :, j], mT[:, 16:32, j], 1.0, vT[:, R_N1:R_N1 + 16], ALU.add, ALU.mult, [B_mT, B_vT], [B_scp])
        for j in range(2):
            stt("dve", sc2p[:, :, j], mT[:, 64:80, j], 1.0, vT[:, R_N2:R_N2 + 16], ALU.add, ALU.mult, [B_mT, B_vT], [B_scp])

        lrow = a2([4096], F32, parts=1)
        B_lrow = S.buf("lrow")
        B_omlD = S.buf("omlD")
        dma("sp", lrow, lbl_d, writes=[B_lrow])
        l4 = lrow.rearrange("p (d s c) -> p d s c", d=2, s=2)
        tt("dve", l4[:, :, 0, :], l4[:, :, 0, :], l4[:, :, 1, :], ALU.subtract, [B_lrow], [B_lrow])
        act(l4[:, :, 1, :], l4[:, :, 0, :], AF.Sigmoid, [B_lrow], [B_lrow], scale=-1.0)
        dma("sp", oml_d.rearrange("o (d c) -> o d c", d=2), l4[:, :, 1, :], reads=[B_lrow], writes=[B_omlD])

        S.phase_fence()
        a2 = Alloc(arena2)
        a1 = Alloc(arena1)
        zT = a2([2048], F32, parts=33)
        fw1 = a2([64], F32, parts=33)
        fw2 = a2([64], F32, parts=64)
        fw3 = a2([2048], F32, parts=64)
        h1T = a2([2048], F32, parts=64)
        h2T = a2([2048], F32, parts=64)
        deltas = a2([2048], F32)
        negt = a2([16], F32)
        ytmp = a2([512], F32, parts=64)
        wtmp = a2([512], F32, parts=64)
        filt = a2([2048], F32)
        sq = a2([2048], F32)
        rn = a2([1024], F32)
        kst = a2([2, 1024], F32)
        ftb = [a2([16, 2, 128], BF16) for _ in range(2)]
        ksum = a1([16, 1024], BF16)
        kdiff = a1([16, 1024], BF16)
        B_f = S.buf("fconst")
        B_h1, B_h2, B_yt, B_wt, B_filt, B_sq, B_rn, B_kst = (S.buf(n) for n in ("h1", "h2", "yt", "wt", "filt", "sq", "rn", "kst"))
        B_ftb = S.bufs("ftb", 2)
        B_ks = S.buf("ksum")
        B_kspD = S.buf("kspD")
        dma("sp", zT, zT_d, writes=[B_f])
        dma("sp", fw1, fw1_d, writes=[B_f])
        dma("sp", fw2, fw2_d, writes=[B_f])
        dma("sp", fw3, fw3_d, writes=[B_f])
        dma("sp", deltas, deltas_d, writes=[B_f])
        dma("sp", negt, negt_d, writes=[B_f])
        FB1, FRQ, FB2 = small[0:64, 0:1], vT[0:64, R_FLT + 1:R_FLT + 2], small[0:64, 1:2]
        tt("dve", FB1, vT[0:64, R_FLT:R_FLT + 1], FRQ, ALU.mult, [B_vT], [B_small])
        tt("dve", FB2, vT[0:64, R_FLT + 2:R_FLT + 3], FRQ, ALU.mult, [B_vT], [B_small])

        def sin_layer(dst, B_dst, lhsT, rhs_full, B_rhs, fb):
            for c4 in range(4):
                pb = banks[c4 % 2]
                cs = slice(c4 * 512, (c4 + 1) * 512)
                mm(pb[0:64, :], lhsT, rhs_full[:, cs], True, True, [B_f, B_rhs], [B_bank[c4 % 2]])
                ts("dve", ytmp, pb[0:64, :], FRQ, fb, ALU.mult, ALU.add, [B_bank[c4 % 2], B_vT, B_small], [B_yt])
                for _ in range(2):
                    ts("dve", wtmp, ytmp, PI, -2 * PI, ALU.is_gt, ALU.mult, [B_yt], [B_wt])
                    tt("dve", ytmp, ytmp, wtmp, ALU.add, [B_yt, B_wt], [B_yt])
                    ts("dve", wtmp, ytmp, -PI, 2 * PI, ALU.is_lt, ALU.mult, [B_yt], [B_wt])
                    tt("dve", ytmp, ytmp, wtmp, ALU.add, [B_yt, B_wt], [B_yt])
                act(dst[:, cs], ytmp, AF.Sin, [B_yt], [B_dst])

        sin_layer(h1T, B_h1, fw1, zT, B_f, FB1)
        sin_layer(h2T, B_h2, fw2, h1T, B_h1, FB2)
        for jt in range(16):
            for c4 in range(4):
                mm(banks[c4][:, :], h2T[:, jt * 128:(jt + 1) * 128], fw3[:, c4 * 512:(c4 + 1) * 512], True, True, [B_h2, B_f], [B_bank[c4]])
            act(sq, deltas, AF.Exp, [B_f], [B_sq], scale=negt[:, jt:jt + 1])
            for c4 in range(4):
                cs = slice(c4 * 512, (c4 + 1) * 512)
                stt("dve", filt[:, cs], sq[:, cs], 0.05, banks[c4][:, :], ALU.add, ALU.mult, [B_sq, B_bank[c4]], [B_filt])
            if jt == 0:
                memset("dve", filt[0:1, 1024:2048], 0.0, [B_filt])
            tt("dve", ksum[:, jt, :], filt[:, 0:1024], filt[:, 1024:2048], ALU.add, [B_filt], [B_ks])
            tt("pool", kdiff[:, jt, :], filt[:, 0:1024], filt[:, 1024:2048], ALU.subtract, [B_filt], [B_ks])
            act(sq, filt, AF.Square, [B_filt], [B_sq])
            for c4 in range(4):
                mm(banks[4 + c4][:, :], ONES, sq[:, c4 * 512:(c4 + 1) * 512], jt == 0, jt == 15, [B_const, B_sq], [B_bank[4 + c4]])
        for c2 in range(2):
            cp("act", rn[:, c2 * 512:(c2 + 1) * 512], banks[4 + c2][:, :], [B_bank[4 + c2]], [B_rn])
            tt("dve", rn[:, c2 * 512:(c2 + 1) * 512], rn[:, c2 * 512:(c2 + 1) * 512], banks[6 + c2][:, :], ALU.add, [B_rn, B_bank[6 + c2]], [B_rn])
        act(rn, rn, AF.Sqrt, [B_rn], [B_rn])
        recip(rn, rn, [B_rn], [B_rn])
        for T in range(16):
            k = T % 2
            dma("sp", ftb[k], FTt_d[T], writes=[B_ftb[k]])
            for ri in range(2):
                src = ksum if ri == 0 else kdiff
                for c2 in range(2):
                    bi = ri * 2 + c2
                    for s_ in range(16):
                        mm(banks[bi][:, :], ftb[k][:, s_, ri, :], src[:, s_, c2 * 512:(c2 + 1) * 512], s_ == 0, s_ == 15, [B_ftb[k], B_ks], [B_bank[bi]])
            if T == 0:
                for c2 in range(2):
                    for s_ in range(16):
                        mm(banks[4 + c2][0:1, :], ftb[k][:, s_, 1, 0:1], ksum[:, s_, c2 * 512:(c2 + 1) * 512], s_ == 0, s_ == 15, [B_ftb[k], B_ks], [B_bank[4 + c2]])
            for ri in range(2):
                for c2 in range(2):
                    bi = ri * 2 + c2
                    tt("dve", kst[:, ri, c2 * 512:(c2 + 1) * 512], banks[bi][:, :], rn[:, c2 * 512:(c2 + 1) * 512], ALU.mult, [B_bank[bi], B_rn], [B_kst])
            if T == 0:
                for c2 in range(2):
                    tt("dve", kst[0:1, 1, c2 * 512:(c2 + 1) * 512], banks[4 + c2][0:1, :], rn[0:1, c2 * 512:(c2 + 1) * 512], ALU.mult, [B_bank[4 + c2], B_rn], [B_kst])
            dma("sp", ksp_d[T], kst, reads=[B_kst], writes=[B_kspD])

        B_mixD = S.buf("mixD")
        for b in range(NB):
            S.phase_fence()
            a1 = Alloc(arena1)
            uT = a1([16, L], BF16)
            ucT = a1([16, CL], BF16)
            B_uT = S.buf("uT")
            a2 = Alloc(arena2)
            xs = [a2([D], F32) for _ in range(2)]
            xn = [a2([D], BF16) for _ in range(2)]
            junk = a2([D], BF16)
            B_xs = S.bufs("xs", 2)
            B_xn = S.bufs("xn", 2)
            B_junk = S.buf("junk")

            def norm_transpose(src_rows, dstT, col0, j, it, B_dst, sc, sh_base):
                k = it % 2
                dma("sp", xs[k], src_rows, writes=[B_xs[k]])
                sm = small[:, 8 + 4 * k: 12 + 4 * k]
                memset("dve", sm[:, 0:1], 0.0, [B_small])
                act(junk, xs[k], AF.Square, [B_xs[k]], [B_junk, B_small], accum=sm[:, 0:1])
                rstd_from_ssq(sm[:, 0:1], sm[:, 1:2], sm[:, 2:3], D, B_small)
                ts("dve", xn[k], xs[k], sm[:, 2:3], None, ALU.mult, None, [B_xs[k], B_small], [B_xn[k]])
                for half in range(2):
                    bi = 2 * k + half
                    pbf = banks[bi][:].bitcast(BF16)
                    for q in range(8):
                        dt = half * 8 + q
                        tr(pbf[:, q * 128:(q + 1) * 128], xn[k][:, dt * 128:(dt + 1) * 128], identb[:], [B_xn[k], B_const], [B_bank[bi]])
                    for q in range(8):
                        dt = half * 8 + q
                        o_ = dstT[:, dt, col0:col0 + 128]
                        i_ = pbf[:, q * 128:(q + 1) * 128]
                        if q % 2 == 0:
                            act(o_, i_, AF.Identity, [B_bank[bi], B_scp, B_mT], [B_dst], scale=sc[:, dt, j:j + 1], bias=mT[:, sh_base + dt, j:j + 1])
                        else:
                            ts("dve", o_, i_, sc[:, dt, j:j + 1], mT[:, sh_base + dt, j:j + 1], ALU.mult, ALU.add, [B_bank[bi], B_scp, B_mT], [B_dst])

            it = 0
            for tl in range(2):
                norm_transpose(ctx_d[b * CL + tl * 128: b * CL + (tl + 1) * 128, :], ucT, tl * 128, 2, it, B_uT, sc1p, 0)
                it += 1
            for tl in range(16):
                norm_transpose(x_d[b * L + tl * 128: b * L + (tl + 1) * 128, :], uT, tl * 128, b, it, B_uT, sc1p, 0)
                it += 1

            S.phase_fence()
            a2 = Alloc(arena2)
            wh = a2([16, 5, 128], BF16)
            omlrow = a2([2, 1024], F32)
            ktok_f = a2([18 * 256], F32)
            lftok_f = a2([18 * 256], F32)
            ktok = ktok_f.rearrange("p (t d c) -> p t d c", t=18, d=2)
            lftok = lftok_f.rearrange("p (t d c) -> p t d c", t=18, d=2)
            vtok = a2([18, 128], BF16)
            qT = a2([L], F32)
            gT = a2([L], F32)
            osum = a2([L], F32)
            etmp = [a2([256], F32) for _ in range(2)]
            Eb = [a2([4, 128], F32) for _ in range(6)]
            Khat = [a2([128], BF16) for _ in range(6)]
            Ktil = [a2([128], BF16) for _ in range(6)]
            KtilT = [a2([128], BF16) for _ in range(6)]
            QtilT = [a2([128], BF16) for _ in range(6)]
            QaT = [a2([128], BF16) for _ in range(6)]
            scm = [a2([128], BF16) for _ in range(6)]
            rtmp = a2([512], F32)
            astage = a2([L], BF16)
            B_wh, B_oml, B_ktok, B_lf, B_vtok, B_qT, B_gT, B_osum, B_rtmp, B_ast = (S.buf(n) for n in
                ("wh", "oml", "ktok", "lf", "vtok", "qT", "gT", "osum", "rtmp", "ast"))
            B_et = S.bufs("et", 2)
            B_E = S.bufs("E", 6)
            B_Kh = S.bufs("Kh", 6)
            B_Kt = S.bufs("Kt", 6)
            B_KtT = S.bufs("KtT", 6)
            B_Qt = S.bufs("Qt", 6)
            B_Qa = S.bufs("Qa", 6)
            B_scm = S.bufs("scm", 6)
            B_S = S.bufs("S", 2)
            B_Sb = S.bufs("Sb", 2)
            B_ct = S.bufs("bct", 2)
            B_cs = S.bufs("bcs", 2)
            B_cu = S.bufs("bcu", 2)
            dma("sp", omlrow, oml_d[0:1, :].rearrange("o (d c) -> o d c", d=2).broadcast_to([128, 2, 1024]), reads=[B_omlD], writes=[B_oml])

            for h in range(8):
                colmaj = h >= 4
                for gi, c0 in enumerate((h * 128, 1024 + h * 128, 2048 + h * 128, 3072 + h * 128, 4096 + h * 128)):
                    dma("sp", wh[:, :, gi, :], winb_d[:, c0:c0 + 128].rearrange("(kt p) n -> p kt n", p=128), reads=[B_winb[c0 // 512]], writes=[B_wh])

                def lat_cols(kt, p0, n):
                    if not colmaj:
                        return uT[:, kt, p0:p0 + n]
                    return uT[:, kt, :].rearrange("p (r c) -> p c r", c=64)[:, p0 // 32:(p0 + n) // 32, :]

                for ti in range(18):
                    pb = banks[ti % 2]
                    for kt in range(16):
                        if ti >= 2 and colmaj:
                            ucm = uT[:, kt, :].rearrange("p (r c) -> p c r", c=64)
                            for j4 in range(4):
                                mm(pb[32 * j4:32 * j4 + 32, 0:384], ucm[:, (ti - 2) * 4 + j4, :], wh[:, kt, 1:4, :], kt == 0, kt == 15,
                                   [B_uT, B_wh], [B_bank[ti % 2]], tp=(0, 32 * j4))
                        else:
                            lhsT = ucT[:, kt, ti * 128:(ti + 1) * 128] if ti < 2 else uT[:, kt, (ti - 2) * 128:(ti - 1) * 128]
                            mm(pb[:, 0:384], lhsT, wh[:, kt, 1:4, :], kt == 0, kt == 15, [B_uT, B_wh], [B_bank[ti % 2]])
                    e_ = etmp[ti % 2]
                    act(e_, pb[:, 0:256], AF.Sigmoid, [B_bank[ti % 2]], [B_et[ti % 2]], scale=-1.0)
                    tt("dve", ktok[:, ti, :, :], e_.rearrange("p (d c) -> p d c", d=2), omlrow[:, :, h * 128:(h + 1) * 128], ALU.mult,
                       [B_et[ti % 2], B_oml], [B_ktok])
                    cp("act", vtok[:, ti, :], pb[:, 256:384], [B_bank[ti % 2]], [B_vtok])
                act(lftok_f, ktok_f, AF.Ln, [B_ktok], [B_lf], scale=-1.0, bias=1.0)
                for gi, (dst, B_dst) in ((0, (qT, B_qT)), (4, (gT, B_gT))):
                    for tc in range(4):
                        pb = banks[2 + tc % 2]
                        for kt in range(16):
                            mm(pb[:, :], wh[:, kt, gi, :], uT[:, kt, tc * 512:(tc + 1) * 512], kt == 0, kt == 15, [B_uT, B_wh], [B_bank[2 + tc % 2]])
                        cp("act" if tc % 2 == 0 else "dve", dst[:, tc * 512:(tc + 1) * 512], pb[:, :], [B_bank[2 + tc % 2]], [B_dst])
                act(gT, gT, AF.Silu, [B_gT], [B_gT])
                memset("pool", osum, 0.0, [B_osum])
                memset("pool", Sst[:], 0.0, [B_S[0], B_S[1]])
                memset("pool", Sbf[:], 0.0, [B_Sb[0], B_Sb[1]])
                order = {0: list(range(18)), 1: [1, 0] + list(range(17, 1, -1))}

                def qslice(p0):
                    if not colmaj:
                        return qT[:, p0:p0 + 128]
                    return qT.rearrange("p (r c) -> p c r", c=64)[:, p0 // 32:p0 // 32 + 4, :]

                def stage1a(step, dr):
                    ti = order[dr][step]
                    is_ctx = ti < 2
                    st_ = 3 * dr + step % 3
                    bA, bC = banks[2 + 3 * dr], banks[4 + 3 * dr]
                    BA = B_bank[2 + 3 * dr]
                    lf = lftok[:, ti, dr, :]
                    kk = ktok[:, ti, dr, :]
                    E = Eb[st_]
                    pA = bA[:].rearrange("p (s c) -> p s c", s=4)
                    if not is_ctx:
                        mm(pA[:, 0, :], lf, cm[:, 2 + dr, :], True, True, [B_lf, B_const], [BA])
                        mm(pA[:, 1, :], cm[:, 4 + dr, :], lf, True, True, [B_lf, B_const], [BA])
                    mm(pA[:, 2, :], lf, cm[:, 0 + dr, :], True, True, [B_lf, B_const], [BA])
                    mm(pA[:, 3, :], cm[:, 6 + dr, :], lf, True, True, [B_lf, B_const], [BA])
                    if is_ctx:
                        act(E[:, 2:4, :], pA[:, 2:4, :], AF.Exp, [BA], [B_E[st_]])
                    else:
                        act(E, pA, AF.Exp, [BA], [B_E[st_]])
                    tt("dve", Khat[st_], kk, E[:, 3, :], ALU.mult, [B_ktok, B_E[st_]], [B_Kh[st_]])
                    if not is_ctx:
                        p0 = (ti - 2) * 128
                        tt("dve", Ktil[st_], kk, E[:, 1, :], ALU.mult, [B_ktok, B_E[st_]], [B_Kt[st_]])
                        qs = qslice(p0)
                        if colmaj:
                            e0 = E[:, 0, :].rearrange("p (c r) -> p c r", r=32)
                            e2 = E[:, 2, :].rearrange("p (c r) -> p c r", r=32)
                            o0 = QtilT[st_].rearrange("p (c r) -> p c r", r=32)
                            o2 = QaT[st_].rearrange("p (c r) -> p c r", r=32)
                        else:
                            e0, e2, o0, o2 = E[:, 0, :], E[:, 2, :], QtilT[st_], QaT[st_]
                        tt("pool", o0, qs, e0, ALU.mult, [B_qT, B_E[st_]], [B_Qt[st_]])
                        tt("pool", o2, qs, e2, ALU.mult, [B_qT, B_E[st_]], [B_Qa[st_]])

                def stage1b(step, dr):
                    ti = order[dr][step]
                    if ti < 2:
                        return
                    st_ = 3 * dr + step % 3
                    bC = banks[4 + 3 * dr]
                    pT = bC[:, 256:384].bitcast(BF16)[:, 0:128]
                    tr(pT, Ktil[st_], identb[:], [B_Kt[st_], B_const], [B_ct[dr]])
                    cp("act", KtilT[st_], pT, [B_ct[dr]], [B_KtT[st_]])
                    mm(bC[:, 0:128], KtilT[st_], QtilT[st_], True, True, [B_KtT[st_], B_Qt[st_]], [B_cs[dr]])
                    tt("dve", scm[st_], bC[:, 0:128], cm[:, 0 + dr, :], ALU.mult, [B_cs[dr], B_const], [B_scm[st_]])

                def stage2(step, dr):
                    ti = order[dr][step]
                    is_ctx = ti < 2
                    st_ = 3 * dr + step % 3
                    bO, bC = banks[3 + 3 * dr], banks[4 + 3 * dr]
                    BO = B_bank[3 + 3 * dr]
                    E = Eb[st_]
                    vv = vtok[:, ti, :]
                    if not is_ctx:
                        mm(bO[:, 0:128], vv, scm[st_], True, False, [B_vtok, B_scm[st_]], [BO])
                    chunks = (0, 1) if dr == 0 else (1, 0)
                    for ci, c in enumerate(chunks):
                        cs = slice(c * 64, (c + 1) * 64)
                        if not is_ctx:
                            mm(bO[:, cs], Sbf[:, dr, :], QaT[st_][:, cs], False, ci == 1, [B_Sb[dr], B_Qa[st_]], [BO])
                        mm(bC[:, 128:256], Khat[st_][cs, :], vtok[cs, ti, :], True, True, [B_Kh[st_], B_vtok], [B_cu[dr]])
                        dcol = (c * 64 + 63) if dr == 0 else (c * 64)
                        stt("dve", Sst[:, dr, :], Sst[:, dr, :], E[:, 2, dcol:dcol + 1], bC[:, 128:256], ALU.mult, ALU.add,
                            [B_S[dr], B_E[st_], B_cu[dr]], [B_S[dr]])
                        cp("act", Sbf[:, dr, :], Sst[:, dr, :], [B_S[dr]], [B_Sb[dr]])
                    if not is_ctx:
                        p0 = (ti - 2) * 128
                        tt("dve", osum[:, p0:p0 + 128], osum[:, p0:p0 + 128], bO[:, 0:128], ALU.add, [B_osum, BO], [B_osum])

                for it in range(20):
                    if it < 18:
                        for dr in range(2):
                            stage1a(it, dr)
                    if 1 <= it < 19:
                        for dr in range(2):
                            stage1b(it - 1, dr)
                    if it >= 2:
                        for dr in range(2):
                            stage2(it - 2, dr)
                act(qT, osum, AF.Square, [B_osum], [B_qT])
                for tc in range(4):
                    pb = banks[tc % 2]
                    cs = slice(tc * 512, (tc + 1) * 512)
                    mm(pb[:, :], ONES, qT[:, cs], True, True, [B_const, B_qT], [B_bank[tc % 2]])
                    act(rtmp, pb[:, :], AF.Sqrt, [B_bank[tc % 2]], [B_rtmp], scale=1.0 / 128, bias=EPS)
                    recip(rtmp, rtmp, [B_rtmp], [B_rtmp])
                    tt("dve", rtmp, rtmp, osum[:, cs], ALU.mult, [B_rtmp, B_osum], [B_rtmp])
                    if colmaj:
                        o_ = astage.rearrange("p (r c) -> p c r", c=64)[:, tc * 16:(tc + 1) * 16, :]
                        i0 = rtmp.rearrange("p (c r) -> p c r", r=32)
                        i1 = gT.rearrange("p (r c) -> p c r", c=64)[:, tc * 16:(tc + 1) * 16, :]
                    else:
                        o_, i0, i1 = astage[:, cs], rtmp, gT[:, cs]
                    stt("dve", o_, i0, vT[:, R_HG + h:R_HG + h + 1], i1, ALU.mult, ALU.mult, [B_rtmp, B_vT, B_gT], [B_ast])
                dma("pool", mix_d[b, h * 128:(h + 1) * 128, :], astage, reads=[B_ast], writes=[B_mixD])
                issue_precast(4)

            for hf in range(2):
                S.phase_fence()
                a2 = Alloc(arena2)
                uuT = a2([4, L], BF16)
                x0T = a2([4, L], BF16)
                uutok = a2([16, 512], BF16)
                mark = a2.off
                whb = [a2([16, 3, 128], BF16) for _ in range(2)]
                pbuf = [a2([L + 2], F32) for _ in range(2)]
                ytm = [a2([L], F32) for _ in range(2)]
                vc = a2([L], F32)
                B_uuT = S.bufs("uuT", 4)
                B_x0T = S.bufs("x0T", 4)
                B_uutok = S.buf("uutok")
                B_whb = S.bufs("whb", 2)
                B_pb = S.bufs("pbuf", 2)
                B_ytm = S.bufs("ytm", 2)
                B_vc = S.buf("vc")
                for k in range(2):
                    memset("pool", pbuf[k][:, 0:1], 0.0, [B_pb[k]])
                    memset("pool", pbuf[k][:, L + 1:L + 2], 0.0, [B_pb[k]])
                si = 0
                for ct in range(4):
                    cg = hf * 4 + ct
                    k = ct % 2
                    for s_ in range(3):
                        c0 = 5120 + s_ * 1024 + cg * 128
                        dma("sp", whb[k][:, :, s_, :], winb_d[:, c0:c0 + 128].rearrange("(kt p) n -> p kt n", p=128), reads=[B_winb[c0 // 512]], writes=[B_whb[k]])
                    for s_ in range(3):
                        pk = si % 2
                        si += 1
                        for tc in range(4):
                            pb = banks[tc % 2]
                            for kt in range(16):
                                mm(pb[:, :], whb[k][:, kt, s_, :], uT[:, kt, tc * 512:(tc + 1) * 512], kt == 0, kt == 15, [B_uT, B_whb[k]], [B_bank[tc % 2]])
                            cp("act", pbuf[pk][:, 1 + tc * 512:1 + (tc + 1) * 512], pb[:, :], [B_bank[tc % 2]], [B_pb[pk]])
                        ch = s_ * 8 + cg
                        w0 = vT[:, R_CW + 0 * 24 + ch:R_CW + 0 * 24 + ch + 1]
                        w1_ = vT[:, R_CW + 1 * 24 + ch:R_CW + 1 * 24 + ch + 1]
                        w2_ = vT[:, R_CW + 2 * 24 + ch:R_CW + 2 * 24 + ch + 1]
                        cb = vT[:, R_CB + ch:R_CB + ch + 1]
                        y_ = vc if s_ == 0 else ytm[pk]
                        B_y = B_vc if s_ == 0 else B_ytm[pk]
                        act(y_, pbuf[pk][:, 1:L + 1], AF.Identity, [B_pb[pk], B_vT], [B_y], scale=w1_, bias=cb)
                        stt("dve", y_, pbuf[pk][:, 0:L], w0, y_, ALU.mult, ALU.add, [B_pb[pk], B_vT, B_y], [B_y])
                        stt("dve", y_, pbuf[pk][:, 2:L + 2], w2_, y_, ALU.mult, ALU.add, [B_pb[pk], B_vT, B_y], [B_y])
                        if s_ == 1:
                            tt("dve", uuT[:, ct, :], y_, vc, ALU.mult, [B_y, B_vc], [B_uuT[ct]])
                        elif s_ == 2:
                            cp("pool", x0T[:, ct, :], y_, [B_y], [B_x0T[ct]])
                    for q4 in range(4):
                        bi = 2 + q4 % 2
                        pbf = banks[bi][:].bitcast(BF16)
                        for q in range(4):
                            s16 = q4 * 4 + q
                            tr(pbf[:, q * 128:(q + 1) * 128], uuT[:, ct, s16 * 128:(s16 + 1) * 128], identb[:], [B_uuT[ct], B_const], [B_bank[bi]])
                        cp("act" if q4 % 2 == 0 else "dve", uutok[:, q4 * 4:(q4 + 1) * 4, ct * 128:(ct + 1) * 128],
                           pbf[:, 0:512].rearrange("p (q c) -> p q c", q=4), [B_bank[bi]], [B_uutok])
                S.phase_fence()
                a2.off = mark
                Y = a2([32, 512], BF16)
                mark2 = a2.off
                ftb = [a2([16, 2, 128], BF16) for _ in range(2)]
                kb = [a2([2, 512], F32) for _ in range(2)]
                tq = [a2([512], F32) for _ in range(4)]
                B_Y = S.buf("Y")
                B_ftb = S.bufs("ftb", 2)
                B_kb = S.bufs("kb", 2)
                B_tq = S.bufs("tq", 4)
                for T in range(16):
                    k = T % 2
                    dma("sp", ftb[k], FTt_d[T], writes=[B_ftb[k]])
                    dma("sp", kb[k], ksp_d[T][:, :, hf * 512:(hf + 1) * 512], reads=[B_kspD], writes=[B_kb[k]])
                    pr, pi_ = banks[2 * k], banks[2 * k + 1]
                    Br, Bi = B_bank[2 * k], B_bank[2 * k + 1]
                    for ri, (pp, Bp) in enumerate(((pr, Br), (pi_, Bi))):
                        for s_ in range(16):
                            mm(pp[:, :], ftb[k][:, s_, ri, :], uutok[:, s_, :], s_ == 0, s_ == 15, [B_ftb[k], B_uutok], [Bp])
                    tt("dve", tq[0], pr[:, :], kb[k][:, 0, :], ALU.mult, [Br, B_kb[k]], [B_tq[0]])
                    tt("dve", tq[1], pi_[:, :], kb[k][:, 1, :], ALU.mult, [Bi, B_kb[k]], [B_tq[1]])
                    tt("dve", tq[2], pr[:, :], kb[k][:, 1, :], ALU.mult, [Br, B_kb[k]], [B_tq[2]])
                    tt("dve", tq[3], pi_[:, :], kb[k][:, 0, :], ALU.mult, [Bi, B_kb[k]], [B_tq[3]])
                    tt("pool", Y[:, T, :], tq[0], tq[1], ALU.subtract, [B_tq[0], B_tq[1]], [B_Y])
                    tt("pool", Y[:, 16 + T, :], tq[2], tq[3], ALU.add, [B_tq[2], B_tq[3]], [B_Y])
                    if T == 0:
                        cp("act", Y[0:1, 0, :], tq[0][0:1, :], [B_tq[0], B_Y], [B_Y])
                        cp("act", Y[0:1, 16, :], tq[1][0:1, :], [B_tq[1], B_Y], [B_Y])
                S.phase_fence()
                a2.off = mark2
                gtb = [a2([32, 256], BF16) for _ in range(2)]
                et = [a2([256], F32) for _ in range(2)]
                B_gtb = S.bufs("gtb", 2)
                B_et2 = S.bufs("et2", 2)
                for tcn in range(8):
                    k = tcn % 2
                    dma("sp", gtb[k], GTt_d[tcn], writes=[B_gtb[k]])
                    for ct in range(4):
                        cg = hf * 4 + ct
                        bi = (tcn * 4 + ct) % 4
                        pb = banks[bi]
                        for gt in range(32):
                            mm(pb[:, 0:256], Y[:, gt, ct * 128:(ct + 1) * 128], gtb[k][:, gt, :], gt == 0, gt == 31, [B_Y, B_gtb[k]], [B_bank[bi]])
                        e2 = et[ct % 2]
                        tsl = slice(tcn * 256, (tcn + 1) * 256)
                        stt("dve", e2, uuT[:, ct, tsl], vT[:, R_HB + cg:R_HB + cg + 1], pb[:, 0:256], ALU.mult, ALU.add,
                            [B_uuT[ct], B_vT, B_bank[bi]], [B_et2[ct % 2]])
                        tt("pool", uuT[:, ct, tsl], e2, x0T[:, ct, tsl], ALU.mult, [B_et2[ct % 2], B_x0T[ct]], [B_uuT[ct]])
                for ct in range(4):
                    cg = hf * 4 + ct
                    dma("pool", mix_d[b, 1024 + cg * 128:1024 + (cg + 1) * 128, :], uuT[:, ct, :], reads=[B_uuT[ct]], writes=[B_mixD])

            issue_precast(len(pending))
            S.phase_fence()
            a1 = Alloc(arena1)
            hT = a1([64, 512], BF16)
            fnrow = a1([D], F32)
            a2 = Alloc(arena2)
            x1 = a2([4, D], F32)
            mcu = a2([16, 512], BF16)
            NWO, NW1, NW2 = 2, 3, 3
            wo = [a2([16, 256], BF16) for _ in range(NWO)]
            w1c = [a2([16, 256], BF16) for _ in range(NW1)]
            w2c = [a2([4, 512], BF16) for _ in range(NW2)]
            xn2 = a2([D], BF16)
            gbuf = a2([2, D], F32)
            tmpd = [a2([512], F32) for _ in range(2)]
            junk = xn2
            B_hT = S.bufs("hT", 64)
            B_fn = S.buf("fnrow")
            B_x1 = S.bufs("x1_", 4)
            B_mcu = S.buf("mcu")
            B_wo = S.bufs("wo", NWO)
            B_w1c = S.bufs("w1c", NW1)
            B_w2c = S.bufs("w2c", NW2)
            B_xn2 = S.buf("xn2")
            B_g = S.buf("gbuf")
            B_tmpd = S.bufs("tmpd", 2)
            B_junk = B_xn2
            dma("sp", fnrow, fng_d[0:1, :].broadcast_to([128, D]), writes=[B_fn])
            dma("sp", gbuf[:, 0, :], mrow_d[b:b + 1, 32 * 128:48 * 128].broadcast_to([128, D]), reads=[B_mrowD], writes=[B_g])
            dma("sp", gbuf[:, 1, :], mrow_d[b:b + 1, 80 * 128:96 * 128].broadcast_to([128, D]), reads=[B_mrowD], writes=[B_g])
            tcount = 0
            for G in range(4):
                t0 = b * L + G * 512
                dma("sp", mcu, mix_d[b, :, G * 512:(G + 1) * 512].rearrange("(ft p) t -> p ft t", p=128), reads=[B_mixD], writes=[B_mcu])
                for tq_ in range(4):
                    dma("sp", x1[:, tq_, :], x_d[t0 + tq_ * 128:t0 + (tq_ + 1) * 128, :], writes=[B_x1[tq_]])
                for dc in range(8):
                    k = (G * 8 + dc) % NWO
                    dsl = slice(dc * 256, (dc + 1) * 256)
                    dma("sp", wo[k], woutb_d[:, dsl].rearrange("(ft p) n -> p ft n", p=128), reads=[B_woutb[dc // 2]], writes=[B_wo[k]])
                    for tq_ in range(4):
                        bi = (dc * 4 + tq_) % 4
                        pb = banks[bi]
                        for ft in range(16):
                            mm(pb[:, 0:256], mcu[:, ft, tq_ * 128:(tq_ + 1) * 128], wo[k][:, ft, :], ft == 0, ft == 15, [B_mcu, B_wo[k]], [B_bank[bi]])
                        td = tmpd[tcount % 2]
                        Btd = B_tmpd[tcount % 2]
                        tcount += 1
                        tt("dve", td[:, 0:256], pb[:, 0:256], gbuf[:, 0, dsl], ALU.mult, [B_bank[bi], B_g], [Btd])
                        tt("pool", x1[:, tq_, dsl], x1[:, tq_, dsl], td[:, 0:256], ALU.add, [B_x1[tq_], Btd], [B_x1[tq_]])
                for tq_ in range(4):
                    sm = small[:, 16 + 4 * (tq_ % 2): 20 + 4 * (tq_ % 2)]
                    memset("dve", sm[:, 0:1], 0.0, [B_small])
                    act(junk, x1[:, tq_, :], AF.Square, [B_x1[tq_]], [B_junk, B_small], accum=sm[:, 0:1])
                    rstd_from_ssq(sm[:, 0:1], sm[:, 1:2], sm[:, 2:3], D, B_small)
                    ts("dve", xn2, x1[:, tq_, :], sm[:, 2:3], None, ALU.mult, None, [B_x1[tq_], B_small], [B_xn2])
                    for half in range(2):
                        bi = 4 + (tq_ * 2 + half) % 4
                        pbf = banks[bi][:].bitcast(BF16)
                        for q in range(8):
                            dt = half * 8 + q
                            tr(pbf[:, q * 128:(q + 1) * 128], xn2[:, dt * 128:(dt + 1) * 128], identb[:], [B_xn2, B_const], [B_bank[bi]])
                        for q in range(8):
                            dt = half * 8 + q
                            o_ = mcu[:, dt, tq_ * 128:(tq_ + 1) * 128]
                            i_ = pbf[:, q * 128:(q + 1) * 128]
                            if q % 2 == 0:
                                act(o_, i_, AF.Identity, [B_bank[bi], B_scp, B_mT], [B_mcu], scale=sc2p[:, dt, b:b + 1], bias=mT[:, 48 + dt, b:b + 1])
                            else:
                                ts("dve", o_, i_, sc2p[:, dt, b:b + 1], mT[:, 48 + dt, b:b + 1], ALU.mult, ALU.add, [B_bank[bi], B_scp, B_mT], [B_mcu])
                for fc in range(32):
                    k = fc % NW1
                    dma("sp", w1c[k], w1b_d[:, fc * 256:(fc + 1) * 256].rearrange("(kt p) n -> p kt n", p=128), reads=[B_w1b[fc // 2]], writes=[B_w1c[k]])
                    for f2 in range(2):
                        fft = fc * 2 + f2
                        bi = fft % 4
                        pb = banks[bi]
                        for kt in range(16):
                            mm(pb[:, :], w1c[k][:, kt, f2 * 128:(f2 + 1) * 128], mcu[:, kt, :], kt == 0, kt == 15, [B_w1c[k], B_mcu], [B_bank[bi]])
                        td = tmpd[tcount % 2]
                        Btd = B_tmpd[tcount % 2]
                        tcount += 1
                        act(td, pb[:, :], AF.Relu, [B_bank[bi]], [Btd])
                        tt("dve" if fft % 2 == 0 else "pool", hT[:, fft, :], td, td, ALU.mult, [Btd], [B_hT[fft]])
                for dc in range(4):
                    dsl = slice(dc * 512, (dc + 1) * 512)
                    pbs = [banks[(dc % 2) * 4 + tq_] for tq_ in range(4)]
                    Bps = [B_bank[(dc % 2) * 4 + tq_] for tq_ in range(4)]
                    for f4 in range(16):
                        k = f4 % NW2
                        dma("sp", w2c[k], w2b_d[f4 * 512:(f4 + 1) * 512, dsl].rearrange("(f p) n -> p f n", p=128), reads=[B_w2b[f4 // 2]], writes=[B_w2c[k]])
                        for f_ in range(4):
                            fft = f4 * 4 + f_
                            for tq_ in range(4):
                                mm(pbs[tq_][:, :], hT[:, fft, tq_ * 128:(tq_ + 1) * 128], w2c[k][:, f_, :], fft == 0, fft == 63,
                                   [B_hT[fft], B_w2c[k]], [Bps[tq_]])
                    for tq_ in range(4):
                        td = tmpd[tcount % 2]
                        Btd = B_tmpd[tcount % 2]
                        tcount += 1
                        tt("dve", td, pbs[tq_][:, :], gbuf[:, 1, dsl], ALU.mult, [Bps[tq_], B_g], [Btd])
                        tt("pool", x1[:, tq_, dsl], x1[:, tq_, dsl], td, ALU.add, [B_x1[tq_], Btd], [B_x1[tq_]])
                for tq_ in range(4):
                    sm = small[:, 24 + 4 * (tq_ % 2): 28 + 4 * (tq_ % 2)]
                    memset("dve", sm[:, 0:1], 0.0, [B_small])
                    act(junk, x1[:, tq_, :], AF.Square, [B_x1[tq_]], [B_junk, B_small], accum=sm[:, 0:1])
                    rstd_from_ssq(sm[:, 0:1], sm[:, 1:2], sm[:, 2:3], D, B_small)
                    stt("dve", x1[:, tq_, :], x1[:, tq_, :], sm[:, 2:3], fnrow, ALU.mult, ALU.mult, [B_x1[tq_], B_small, B_fn], [B_x1[tq_]])
                    dma("pool", out_d[t0 + tq_ * 128:t0 + (tq_ + 1) * 128, :], x1[:, tq_, :], reads=[B_x1[tq_]], final=True)

        S.emit(st)
        build_program.stats = S.stats
    return nc


_PROG = {}


def _layout_inputs(inp):
    f32 = lambda a: np.ascontiguousarray(np.asarray(a, dtype=np.float32))
    c = _consts()
    x = f32(inp["x"])
    ctx = f32(inp["ctx"])
    cvec = f32(inp["c"])
    cctx = f32(inp["c_ctx"])
    shared = dict(
        w_ada=f32(inp["w_ada"])[0], b_ada=f32(inp["b_ada"])[0].reshape(1, -1),
        lbl=f32(inp["hgrn_lb_logits"]).reshape(1, 4096),
        w_in=f32(inp["w_in"])[0], w_out=f32(inp["w_out"])[0], w_mlp1=f32(inp["w_mlp1"])[0], w_mlp2=f32(inp["w_mlp2"])[0],
        flt_w1=f32(inp["flt_w1"])[0], flt_w2=f32(inp["flt_w2"])[0], flt_w3=f32(inp["flt_w3"])[0],
        fng=f32(inp["final_norm_g"]).reshape(1, -1),
        identb=c["identb"], identf=c["identf"], cmats=c["cmats"], zT=c["zT"], deltas=c["deltas"], negt=c["negt"],
        FTt=c["FTt"], GTt=c["GTt"],
    )
    rows = np.zeros((NROWS, 128), np.float32)
    rows[R_BADA:R_BADA + 96] = f32(inp["b_ada"])[0].reshape(96, 128)
    rows[R_N1:R_N1 + 16] = f32(inp["norm1_g"])[0].reshape(16, 128)
    rows[R_LB:R_LB + 32] = f32(inp["hgrn_lb_logits"]).reshape(32, 128)
    rows[R_HG:R_HG + 8] = f32(inp["hgrn_norm_g"])[0].reshape(8, 128)
    rows[R_CW:R_CW + 72] = f32(inp["hy_conv_w"])[0].reshape(72, 128)
    rows[R_CB:R_CB + 24] = f32(inp["hy_conv_b"])[0].reshape(24, 128)
    rows[R_HB:R_HB + 8] = f32(inp["hy_bias"])[0].reshape(8, 128)
    rows[R_N2:R_N2 + 16] = f32(inp["norm2_g"])[0].reshape(16, 128)
    rows[R_FN:R_FN + 16] = f32(inp["final_norm_g"]).reshape(16, 128)
    rows[R_FLT + 0, 0:64] = f32(inp["flt_b1"])[0]
    rows[R_FLT + 1, 0:64] = f32(inp["flt_freq"])[0]
    rows[R_FLT + 2, 0:64] = f32(inp["flt_b2"])[0]
    maps = []
    for core in range(8):
        r = rows.copy()
        for j in range(2):
            r[R_C + j * 16:R_C + (j + 1) * 16] = cvec[core * NB + j].reshape(16, 128)
        r[R_C + 32:R_C + 48] = cctx.reshape(16, 128)
        m = dict(shared)
        m["x"] = x[core * NB:(core + 1) * NB].reshape(NB * L, D)
        m["ctx"] = ctx[core * NB:(core + 1) * NB].reshape(NB * CL, D)
        m["vecs"] = r
        maps.append(m)
    return maps


def kernel(**inputs):
    if "nc" not in _PROG:
        _PROG["nc"] = build_program(DEBUG)
    nc = _PROG["nc"]
    maps = _layout_inputs(inputs)
    res = run_bass_kernel_spmd(nc, maps, core_ids=list(range(8)))
    if DEBUG:
        kernel.last = res
    out = np.concatenate([np.asarray(r["out"]).reshape(NB, L, D) for r in res.results], axis=0)
    return out.astype(np.float32)
```
